# Optimizing a Trainium2 kernel written in Bass

```python
import math
import jax, jax.numpy as jnp
from jax import lax
import numpy as np

D_MODEL = 1024
BATCH = 16
SEQ = 2048
DEPTH = 1

N_HEADS = 16
N_KV_HEADS = 4
HEAD_DIM = 64
ATTN_WIDTH = N_HEADS * HEAD_DIM
KV_WIDTH = N_KV_HEADS * HEAD_DIM
WINDOW = 128
BLOCK = 128
REL_BUCKETS = 32
REL_MAX_DIST = 128
NEG_INF = -1e30
HYENA_WIDTH = 1024
HYENA_ORDER = 2
SHORT_CONV = 3
FILTER_EMB = 33
FILTER_HIDDEN = 64
DECAY_FAST = 0.3
DECAY_SLOW = 1.5
DECAY_TARGET = 1e-2
N_BRANCHES = 2
SPLIT_SIZES = (ATTN_WIDTH, KV_WIDTH, KV_WIDTH, ATTN_WIDTH,
               (HYENA_ORDER + 1) * HYENA_WIDTH, HYENA_WIDTH, N_BRANCHES * D_MODEL)
IN_COLS = ATTN_WIDTH + 2 * KV_WIDTH + ATTN_WIDTH + (HYENA_ORDER + 1) * HYENA_WIDTH + HYENA_WIDTH + N_BRANCHES * D_MODEL
DEEPNORM_ALPHA = (2 * DEPTH) ** 0.25
DEEPNORM_BETA = (8 * DEPTH) ** -0.25
LN_EPS = 1e-5

kernel_name = "hybrid_swa_hyena_deepnorm_encoder"


def _t5_bucket(rel):
    half = REL_BUCKETS // 2
    max_exact = half // 2
    ret = (rel > 0).astype(np.int32) * half
    n = np.abs(rel)
    n_safe = np.maximum(n, 1).astype(np.float32)
    large = max_exact + (np.log(n_safe / max_exact) / math.log(REL_MAX_DIST / max_exact)
                         * (half - max_exact)).astype(np.int32)
    large = np.minimum(large, half - 1)
    return (ret + np.where(n < max_exact, n, large)).astype(np.int32)


def _layer_norm(x, g, b):
    xf = x.astype(jnp.float32)
    mu = xf.mean(-1, keepdims=True)
    var = jnp.square(xf - mu).mean(-1, keepdims=True)
    y = (xf - mu) * lax.rsqrt(var + LN_EPS) * g.astype(jnp.float32) + b.astype(jnp.float32)
    return y.astype(x.dtype)


def _windowed_gqa(q, k, v, rel_bias, sink):
    b, s = q.shape[0], q.shape[1]
    nb = s // BLOCK
    g = N_HEADS // N_KV_HEADS
    scale = HEAD_DIM ** -0.5
    q = q.reshape(b, s, N_KV_HEADS, g, HEAD_DIM)
    pad = ((0, 0), (BLOCK, BLOCK), (0, 0), (0, 0))
    kp = jnp.pad(k, pad)
    vp = jnp.pad(v, pad)
    a = np.arange(BLOCK)[:, None]
    c = np.arange(3 * BLOCK)[None, :]
    rel = c - BLOCK - a
    band = jnp.asarray(np.abs(rel) <= WINDOW)
    bias = rel_bias.astype(jnp.float32)[_t5_bucket(rel)]
    bias = bias.transpose(2, 0, 1).reshape(N_KV_HEADS, g, BLOCK, 3 * BLOCK)
    sink_l = sink.astype(jnp.float32).reshape(1, N_KV_HEADS, g, 1, 1)
    offs = jnp.arange(3 * BLOCK)

    def attend_block(n):
        start = n * BLOCK
        qb = lax.dynamic_slice_in_dim(q, start, BLOCK, axis=1)
        kb = lax.dynamic_slice_in_dim(kp, start, 3 * BLOCK, axis=1)
        vb = lax.dynamic_slice_in_dim(vp, start, 3 * BLOCK, axis=1)
        key_pos = start - BLOCK + offs
        valid = band & ((key_pos >= 0) & (key_pos < s))[None, :]
        sc = jnp.einsum('bqkgd,bckd->bkgqc', qb, kb,
                        preferred_element_type=jnp.float32) * scale + bias
        sc = jnp.where(valid, sc, NEG_INF)
        m = jnp.maximum(sc.max(-1, keepdims=True), sink_l)
        e = jnp.exp(sc - m)
        p = e / (e.sum(-1, keepdims=True) + jnp.exp(sink_l - m))
        return jnp.einsum('bkgqc,bckd->bqkgd', p.astype(vb.dtype), vb)

    out = lax.map(attend_block, jnp.arange(nb))
    return out.transpose(1, 0, 2, 3, 4, 5).reshape(b, s, ATTN_WIDTH)


def _short_conv(u, w, bias):
    s = u.shape[1]
    half = SHORT_CONV // 2
    up = jnp.pad(u, ((0, 0), (half, half), (0, 0)))
    y = bias
    for j in range(SHORT_CONV):
        y = y + up[:, j:j + s] * w[j]
    return y


def _hyena_filters(seq_len, w1, b1, w2, b2, w3, b3, w4, freq):
    f32 = jnp.float32
    bands = (FILTER_EMB - 1) // 2
    t = jnp.linspace(0.0, 1.0, seq_len, dtype=f32)[:, None]
    w = 2.0 * math.pi * jnp.arange(seq_len, dtype=f32)[:, None] / seq_len
    fb = jnp.linspace(1e-4, bands - 1, bands, dtype=f32)[None, :]
    z = jnp.concatenate([t, jnp.cos(fb * w), -jnp.sin(fb * w)], axis=-1)
    fr = freq.astype(f32)
    h = jnp.sin(fr * (z @ w1.astype(f32) + b1.astype(f32)))
    h = jnp.sin(fr * (h @ w2.astype(f32) + b2.astype(f32)))
    h = jnp.sin(fr * (h @ w3.astype(f32) + b3.astype(f32)))
    h = h @ w4.astype(f32)
    max_decay = math.log(DECAY_TARGET) / DECAY_FAST
    min_decay = math.log(DECAY_TARGET) / DECAY_SLOW
    deltas = jnp.linspace(min_decay, max_decay, HYENA_WIDTH, dtype=f32)
    decay = jnp.exp(-t * jnp.abs(deltas))
    h = h.reshape(seq_len, 2, HYENA_WIDTH) * decay[:, None, :]
    return h[:, 0], h[:, 1]


def _bidir_long_conv(u, h_fwd, h_bwd):
    L = u.shape[1]
    k = jnp.concatenate([h_fwd, jnp.zeros((1, h_fwd.shape[1]), h_fwd.dtype), h_bwd[:0:-1]], axis=0)
    K = jnp.fft.rfft(k, axis=0)
    U = jnp.fft.rfft(u.astype(jnp.float32), n=2 * L, axis=1)
    y = jnp.fft.irfft(U * K[None], n=2 * L, axis=1)[:, :L]
    return y.astype(u.dtype)


def _hybrid_layer(x, w_in, rel_bias, attn_sink, conv_w, conv_b,
                  filt_w1, filt_b1, filt_w2, filt_b2, filt_w3, filt_b3, filt_w4, filt_freq,
                  hyena_skip, w_branch_attn, w_branch_hyena, w_out, ln_g, ln_b):
    b, s, _ = x.shape
    u = x @ w_in
    idx = np.cumsum(SPLIT_SIZES)[:-1].tolist()
    q, k, v, a_gate, hy, h_gate, br_gate = jnp.split(u, idx, axis=-1)
    y_a = _windowed_gqa(q.reshape(b, s, N_HEADS, HEAD_DIM),
                        k.reshape(b, s, N_KV_HEADS, HEAD_DIM),
                        v.reshape(b, s, N_KV_HEADS, HEAD_DIM), rel_bias, attn_sink)
    y_a = y_a * jax.nn.silu(a_gate)
    hc = _short_conv(hy, conv_w, conv_b)
    x0, x1, hv = jnp.split(hc, HYENA_ORDER + 1, axis=-1)
    z = hv * x1
    h_f, h_b = _hyena_filters(s, filt_w1, filt_b1, filt_w2, filt_b2, filt_w3, filt_b3, filt_w4, filt_freq)
    z = _bidir_long_conv(z, h_f, h_b) + z * hyena_skip
    y_h = z * x0 * jax.nn.silu(h_gate)
    g_a, g_h = jnp.split(jax.nn.sigmoid(br_gate), N_BRANCHES, axis=-1)
    merged = g_a * (y_a @ w_branch_attn) + g_h * (y_h @ w_branch_hyena)
    out = merged @ w_out
    return _layer_norm(DEEPNORM_ALPHA * x + out, ln_g, ln_b)


def setup_inputs(seed: int = 0) -> dict:
    key = jax.random.key(seed)
    ks = jax.random.split(key, 24)
    f32 = jnp.float32
    nrm = lambda k, shape, sc: jax.random.normal(k, shape, f32) * sc
    L_ = DEPTH
    C3 = (HYENA_ORDER + 1) * HYENA_WIDTH
    return {
        "x": nrm(ks[0], (BATCH, SEQ, D_MODEL), 1.0),
        "w_in": nrm(ks[1], (L_, D_MODEL, IN_COLS), D_MODEL ** -0.5),
        "rel_bias": nrm(ks[2], (REL_BUCKETS, N_HEADS), 0.5),
        "attn_sink": nrm(ks[3], (L_, N_HEADS), 0.5),
        "conv_w": nrm(ks[4], (L_, SHORT_CONV, C3), SHORT_CONV ** -0.5),
        "conv_b": nrm(ks[5], (L_, C3), 0.01),
        "filt_w1": nrm(ks[6], (L_, FILTER_EMB, FILTER_HIDDEN), FILTER_EMB ** -0.5),
        "filt_b1": nrm(ks[7], (L_, FILTER_HIDDEN), 0.1),
        "filt_w2": nrm(ks[8], (L_, FILTER_HIDDEN, FILTER_HIDDEN), FILTER_HIDDEN ** -0.5),
        "filt_b2": nrm(ks[9], (L_, FILTER_HIDDEN), 0.1),
        "filt_w3": nrm(ks[10], (L_, FILTER_HIDDEN, FILTER_HIDDEN), FILTER_HIDDEN ** -0.5),
        "filt_b3": nrm(ks[11], (L_, FILTER_HIDDEN), 0.1),
        "filt_w4": nrm(ks[12], (L_, FILTER_HIDDEN, 2 * HYENA_WIDTH), 0.05 * FILTER_HIDDEN ** -0.5),
        "filt_freq": 1.0 + nrm(ks[13], (L_, FILTER_HIDDEN), 0.05),
        "hyena_skip": nrm(ks[14], (L_, HYENA_WIDTH), 1.0),
        "w_branch_attn": nrm(ks[15], (L_, ATTN_WIDTH, D_MODEL), DEEPNORM_BETA * ATTN_WIDTH ** -0.5),
        "w_branch_hyena": nrm(ks[16], (L_, HYENA_WIDTH, D_MODEL), DEEPNORM_BETA * HYENA_WIDTH ** -0.5),
        "w_out": nrm(ks[17], (L_, D_MODEL, D_MODEL), DEEPNORM_BETA * D_MODEL ** -0.5),
        "ln_g": 1.0 + nrm(ks[18], (L_, D_MODEL), 0.01),
        "ln_b": nrm(ks[19], (L_, D_MODEL), 0.01),
    }


def reference(x, w_in, rel_bias, attn_sink, conv_w, conv_b, filt_w1, filt_b1, filt_w2, filt_b2,
              filt_w3, filt_b3, filt_w4, filt_freq, hyena_skip, w_branch_attn, w_branch_hyena,
              w_out, ln_g, ln_b):
    h = x
    for l in range(DEPTH):
        h = _hybrid_layer(h, w_in[l], rel_bias, attn_sink[l], conv_w[l], conv_b[l],
                          filt_w1[l], filt_b1[l], filt_w2[l], filt_b2[l], filt_w3[l], filt_b3[l],
                          filt_w4[l], filt_freq[l], hyena_skip[l], w_branch_attn[l],
                          w_branch_hyena[l], w_out[l], ln_g[l], ln_b[l])
    return h
```

```python
import math
import contextlib
import numpy as np
import ml_dtypes
import concourse.bass as bass
import concourse.mybir as mybir
from concourse.bass_utils import run_bass_kernel_spmd

F32 = mybir.dt.float32
BF16 = mybir.dt.bfloat16
AF = mybir.ActivationFunctionType
ALU = mybir.AluOpType

D = 1024
S = 2048
NCORE = 8
NSEQ = 2
ALPHA = 2.0 ** 0.25
LN_EPS = 1e-5
NEG = -30000.0
PI = math.pi
NW = 84
DEBUG = False
NWB = 5


class Op:
    __slots__ = ("eng", "fn", "deps", "sig", "sigval", "is_dma", "dsem", "dval", "gi")

    def __init__(self, eng, fn, is_dma, gi):
        self.eng = eng
        self.fn = fn
        self.is_dma = is_dma
        self.deps = ()
        self.sig = False
        self.sigval = 0
        self.dsem = None
        self.dval = 0
        self.gi = gi


class Reg:
    __slots__ = ("ap", "keys")

    def __init__(self, ap, keys):
        self.ap = ap
        self.keys = list(keys)

    def rows(self, p0, p1):
        return Reg(self.ap[p0:p1], self.keys)


class Sched:
    STREAMS = ("pe", "act", "dve", "pool", "sp")

    def __init__(self):
        self.ops = {e: [] for e in self.STREAMS}
        self.last_w = {}
        self.readers = {}
        self.n = 0

    def add(self, eng, fn, reads=(), writes=(), dma=False):
        op = Op(eng, fn, dma, self.n)
        self.n += 1
        deps = set()
        for k in reads:
            w = self.last_w.get(k)
            if w is not None:
                deps.add(w)
        for k in writes:
            w = self.last_w.get(k)
            if w is not None:
                deps.add(w)
            rd = self.readers.get(k)
            if rd:
                deps.update(rd.values())
        if eng == "pe" and not dma:
            deps = {d for d in deps if d.is_dma or d.eng != "pe"}
        op.deps = tuple(deps)
        for d in deps:
            d.sig = True
        for k in writes:
            self.last_w[k] = op
            self.readers[k] = {}
        for k in reads:
            rd = self.readers.setdefault(k, {})
            rk = ("dma", op.gi) if dma else eng
            rd[rk] = op
        self.ops[eng].append(op)
        return op

    def emit(self, nc, block, sems, dma_sems):
        for e in self.STREAMS:
            c = 0
            for op in self.ops[e]:
                if not op.is_dma and op.sig:
                    c += 1
                    op.sigval = c
        for e in self.STREAMS:
            pool = dma_sems.get(e)
            i = 0
            for op in self.ops[e]:
                if op.is_dma:
                    op.dsem = pool[i % len(pool)]
                    op.dval = 16 * (i // len(pool) + 1)
                    i += 1

        def run_stream(e, eng):
            waited = {}

            def wait(sem, val):
                if waited.get(sem.num if hasattr(sem, "num") else id(sem), 0) < val:
                    eng.wait_ge(sem, val)
                    waited[sem.num if hasattr(sem, "num") else id(sem)] = val

            for op in self.ops[e]:
                for d in op.deps:
                    if d.is_dma:
                        wait(d.dsem, d.dval)
                    else:
                        wait(sems[d.eng], d.sigval)
                if op.is_dma:
                    if op.dval > 16:
                        wait(op.dsem, op.dval - 16)
                    op.fn(eng).then_inc(op.dsem, 16)
                else:
                    ins = op.fn(eng)
                    if op.sig:
                        ins.then_inc(sems[e], 1)
            last = {}
            for op in self.ops[e]:
                if op.is_dma:
                    last[id(op.dsem)] = (op.dsem, op.dval)
            for sem, val in last.values():
                wait(sem, val)

        @block.tensor
        def _(eng):
            run_stream("pe", eng)

        @block.scalar
        def _(eng):
            run_stream("act", eng)

        @block.vector
        def _(eng):
            run_stream("dve", eng)

        @block.gpsimd
        def _(eng):
            run_stream("pool", eng)

        @block.sync
        def _(eng):
            run_stream("sp", eng)


class Arena:
    def __init__(self, name, t, ncols):
        self.name = name
        self.t = t
        self.t32 = t[:].bitcast(F32)
        self.ncols = ncols

    def keys(self, c0, c1):
        return [(self.name, u) for u in range(c0 // 512, (c1 - 1) // 512 + 1)]

    def bf(self, c0, c1, p0=0, p1=128):
        return Reg(self.t[p0:p1, c0:c1], self.keys(c0, c1))

    def f32(self, c0, c1, p0=0, p1=128):
        return Reg(self.t32[p0:p1, c0:c1], self.keys(2 * c0, 2 * c1))

    def bfv(self, c0, c1, inner, a0, a1, b0, b1, p0=0, p1=128):
        v = self.t[p0:p1, c0:c1].rearrange("p (a b) -> p a b", b=inner)[:, a0:a1, b0:b1]
        return Reg(v, self.keys(c0, c1))


def build_program():
    nc = bass.Bass("TRN2", target_bir_lowering=False)
    dt = nc.dram_tensor

    def din(name, shape, dtype=F32):
        return dt(name, list(shape), dtype, kind="ExternalInput").ap()

    xT_d = din("xT", [NSEQ, 128, 8, S])
    xr_d = din("xr", [NSEQ * S, D])
    wst_d = din("wst", [NW, 128, 8, 128])
    wout_d = din("wout", [128, 8, D])
    cw_d = din("cw", [128, 72])
    cb_d = din("cb", [128, 24])
    esin_d = din("esin", [128, 1024])
    bt_d = din("bt", [2, 2, 3, 128, 512])
    lng_d = din("lng", [128, D])
    lnb_d = din("lnb", [128, D])
    fz_d = din("fz", [33, 4096])
    fwp_d = din("fwp", [64, 2240])
    fsm_d = din("fsm", [64, 4])
    dl_d = din("dl", [128, 1024])
    tv_d = din("tv", [128, 32])
    skip_d = din("skip", [1, 1024])
    F2_d = din("F2", [32, 128, 8, 128], BF16)
    G2_d = din("G2", [2, 2, 2, 128, 8, 512], BF16)
    ident_d = din("ident", [128, 128], BF16)
    aid_d = din("aid", [128, 128])
    jm_d = din("jm", [128, 256], BF16)
    rot_d = din("rot", [128, 64])
    out_d = dt("out", [NSEQ * S, D], F32, kind="ExternalOutput").ap()
    ksp_d = dt("kspec", [16, 3, 128, 1024], BF16, kind="Internal").ap()
    dbg_d = dt("dbg", [4, 128, 16384], BF16, kind="ExternalOutput").ap() if DEBUG else None

    sc = Sched()
    es = contextlib.ExitStack()
    with es:
        def sb(name, shape, dtype):
            return es.enter_context(nc.sbuf_tensor(name, list(shape), dtype))

        XT = Arena("xT", sb("xT_s", [128, 16384], BF16), 16384)
        YH = Arena("yh", sb("yh_s", [128, 16384], BF16), 16384)
        YA = Arena("ya", sb("ya_s", [128, 16384], BF16), 16384)
        S1 = Arena("S1", sb("S1_s", [128, 16384], BF16), 16384)
        S2 = Arena("S2", sb("S2_s", [128, 8192], BF16), 8192)
        S3 = Arena("S3", sb("S3_s", [128, 9216], BF16), 9216)
        W32 = Arena("W32", sb("W32_s", [128, 8192], BF16), 8192)
        WB = Arena("wb", sb("wb_s", [128, NWB * 1024], BF16), NWB * 1024)
        es_t = sb("es_s", [128, 1024], F32)
        lng_t = sb("lng_s", [128, D], F32)
        lnb_t = sb("lnb_s", [128, D], F32)
        cw_t = sb("cw_s", [128, 72], F32)
        cb_t = sb("cb_s", [128, 24], F32)
        ident_t = sb("ident_s", [128, 128], BF16)
        ones_t = sb("ones_s", [128, 256], BF16)
        fsm_t = sb("fsm_s", [64, 8], F32)
        tv_t = sb("tv_s", [128, 32], F32)
        st_t = sb("st_s", [128, 64], F32)
        ps_all = es.enter_context(nc.psum_tensor("ps_all", [128, 4096], F32))
        ps = [ps_all[:, b * 512:(b + 1) * 512] for b in range(8)]

        def PS(b, p0=0, p1=128, c0=0, c1=512):
            return Reg(ps_all[p0:p1, b * 512 + c0:b * 512 + c1], [("ps", b)])

        def PSW(b0, nb_, p0=0, p1=128):
            return Reg(ps_all[p0:p1, b0 * 512:(b0 + nb_) * 512], [("ps", b) for b in range(b0, b0 + nb_)])

        def small(t, name, c0, c1, p0=0, p1=None):
            p1 = t.shape[0] if p1 is None else p1
            return Reg(t[p0:p1, c0:c1], [(name,)])

        def mm(out, lhsT, rhs, start, stop):
            sc.add("pe", lambda e, o=out.ap, l=lhsT.ap, r=rhs.ap, a=start, b=stop:
                   e.matmul(o, lhsT=l, rhs=r, start=a, stop=b),
                   reads=lhsT.keys + rhs.keys, writes=out.keys)

        def act(out, in_, func, scale=1.0, bias=0.0, extra_reads=()):
            sc.add("act", lambda e, o=out.ap, i=in_.ap, f=func, s=scale, b=bias:
                   e.activation(out=o, in_=i, func=f, bias=b, scale=s),
                   reads=in_.keys + list(extra_reads), writes=out.keys)

        def tt(eng, out, in0, in1, op):
            sc.add(eng, lambda e, o=out.ap, a=in0.ap, b=in1.ap, p=op:
                   e.tensor_tensor(out=o, in0=a, in1=b, op=p),
                   reads=in0.keys + in1.keys, writes=out.keys)

        def ts(eng, out, in0, s1, s2, op0, op1=None, extra_reads=()):
            if op1 is None:
                sc.add(eng, lambda e, o=out.ap, a=in0.ap, x=s1, p0=op0:
                       e.tensor_scalar(out=o, in0=a, scalar1=x, scalar2=None, op0=p0),
                       reads=in0.keys + list(extra_reads), writes=out.keys)
            else:
                sc.add(eng, lambda e, o=out.ap, a=in0.ap, x=s1, y=s2, p0=op0, p1=op1:
                       e.tensor_scalar(out=o, in0=a, scalar1=x, scalar2=y, op0=p0, op1=p1),
                       reads=in0.keys + list(extra_reads), writes=out.keys)

        def stt(out, in0, scalar, in1, op0, op1, extra_reads=()):
            sc.add("dve", lambda e, o=out.ap, a=in0.ap, s=scalar, b=in1.ap, p0=op0, p1=op1:
                   e.scalar_tensor_tensor(out=o, in0=a, scalar=s, in1=b, op0=p0, op1=p1),
                   reads=in0.keys + in1.keys + list(extra_reads), writes=out.keys)

        def cp(eng, out, in_):
            if eng == "act":
                act(out, in_, AF.Identity)
                return
            sc.add(eng, lambda e, o=out.ap, i=in_.ap: e.tensor_copy(out=o, in_=i),
                   reads=in_.keys, writes=out.keys)

        def memset(eng, out, val):
            sc.add(eng, lambda e, o=out.ap, v=val: e.memset(o, v), reads=(), writes=out.keys)

        def dma(q, out_ap, in_ap, reads, writes, **kw):
            sc.add(q, lambda e, o=out_ap, i=in_ap, k=kw: e.dma_start(out=o, in_=i, **k),
                   reads=reads, writes=writes, dma=True)

        dma("sp", cw_t[:], cw_d, [], [("cw",)])
        dma("sp", cb_t[:], cb_d, [], [("cb",)])
        dma("sp", ident_t[:], ident_d, [], [("ident",)])
        dma("sp", lng_t[:], lng_d, [], [("lng",)])
        dma("sp", lnb_t[:], lnb_d, [], [("lnb",)])
        dma("sp", es_t[:], esin_d, [], [("es",)])
        dma("sp", fsm_t[:, 0:4], fsm_d, [], [("fsm",)])
        dma("sp", tv_t[:], tv_d, [], [("tv",)])
        SKIP = XT.f32(0, 1024, 0, 1)
        ZROW = XT.bf(2048, 3072, 0, 1)
        dma("sp", SKIP.ap, skip_d, [], SKIP.keys)
        IDENT = Reg(ident_t[:], [("ident",)])
        aid_t = sb("aid_s", [128, 128], F32)
        dma("sp", aid_t[:], aid_d, [], [("aid",)])
        AID = Reg(aid_t[:], [("aid",)])
        jm_t = sb("jm_s", [128, 256], BF16)
        dma("sp", jm_t[:], jm_d, [], [("jm",)])
        JMAT = Reg(jm_t[:, 0:128], [("jm",)])
        NJMAT = Reg(jm_t[:, 128:256], [("jm",)])
        rot_t = sb("rot_s", [128, 64], F32)
        dma("sp", rot_t[:], rot_d, [], [("rot",)])
        memset("pool", Reg(ones_t[:, 0:64], [("ones",)]), 1.0)
        memset("pool", Reg(ones_t[:, 64:192], [("ones",)]), 0.0)
        memset("pool", Reg(ones_t[:, 192:256], [("ones",)]), 1.0)
        ONES = [Reg(ones_t[:, 0:128], [("ones",)]), Reg(ones_t[:, 128:256], [("ones",)])]
        memset("pool", ZROW, 0.0)
        act(Reg(es_t[:], [("es",)]), Reg(es_t[:], [("es",)]), AF.Exp)

        wstate = {"n": 0}

        def load_w_raw(idx):
            slot = wstate["n"] % NWB
            wstate["n"] += 1
            r = WB.bf(slot * 1024, slot * 1024 + 1024)
            dma("pool", r.ap.rearrange("p (a b) -> p a b", b=128), wst_d[idx], [], r.keys)
            return slot

        preloaded = {}

        def preload_w(idxs):
            for idx in idxs:
                if idx not in preloaded:
                    preloaded[idx] = load_w_raw(idx)

        def load_w(idx):
            if idx in preloaded:
                return preloaded.pop(idx)
            return load_w_raw(idx)

        def wchunk(slot, dc):
            return WB.bf(slot * 1024 + dc * 128, slot * 1024 + dc * 128 + 128)

        def wchunk_all(slot):
            return WB.bf(slot * 1024, slot * 1024 + 1024)

        def xTr(dc, t0, t1):
            return XT.bf(dc * 2048 + t0, dc * 2048 + t1)

        fstate = {"n": 0}


        def Fblk(b, s_c):
            return S3.bf(b * 1024 + s_c * 128, b * 1024 + s_c * 128 + 128)

        def load_F2(fc):
            b = fstate["n"] % 6
            fstate["n"] += 1
            r = S3.bf(b * 1024, b * 1024 + 1024)
            dma("sp", r.ap.rearrange("p (a b) -> p a b", b=128), F2_d[fc], [], r.keys)
            return b

        def rev(t, c0, n):
            return t[:, c0 + n - 1:(c0 - 1 if c0 > 0 else None):-1]

        def filter_phase():
            S2f = S2
            dma("sp", S2f.f32(0, 2240, 0, 64).ap, fwp_d, [], S2f.f32(0, 2240).keys)
            dma("sp", W32.f32(0, 4096, 0, 33).ap, fz_d, [], W32.f32(0, 4096).keys)
            dma("sp", S3.f32(0, 1024).ap, dl_d, [], S3.f32(0, 1024).keys)
            FSM = [("fsm",)]
            for l in range(3):
                ts("dve", Reg(fsm_t[:, 4 + l:5 + l], FSM), Reg(fsm_t[:, l:l + 1], FSM), fsm_t[:, 3:4], None, ALU.mult)
            W4B = XT.bf(4096, 6144, 0, 64)
            dma("pool", W4B.ap, fwp_d[:, 0:2048], [], W4B.keys)

            def H16(d):
                return XT.bf(6144 + d * 2048, 6144 + d * 2048 + 2048, 0, 64)
            W4 = lambda c0, c1: Reg(W4B.ap[:, c0:c1], W4B.keys)
            W1 = S2f.f32(2048, 2112, 0, 33)
            W2 = S2f.f32(2112, 2176, 0, 64)
            W3 = S2f.f32(2176, 2240, 0, 64)
            def Hb(d, c0, c1):
                return YH.f32(d * 2048 + c0, d * 2048 + c1, 0, 64)
            bank = {"n": 0}

            def nb():
                b = bank["n"] % 8
                bank["n"] += 1
                return b
            for layer in range(3):
                for d in range(2):
                    b0 = d * 4
                    for blk in range(4):
                        c0, c1 = blk * 512, blk * 512 + 512
                        if layer == 0:
                            mm(PS(b0 + blk, 0, 64), W1, W32.f32(d * 2048 + c0, d * 2048 + c1, 0, 33), True, True)
                        else:
                            mm(PS(b0 + blk, 0, 64), W2 if layer == 1 else W3, Hb(d, c0, c1), True, True)
                    TA = YH.f32(4096, 6144, 0, 64) if d == 0 else S1.f32(0, 2048, 0, 64)
                    TB = YH.f32(6144, 8192, 0, 64) if d == 0 else S1.f32(2048, 4096, 0, 64)
                    ts("dve", TA, PSW(b0, 4, 0, 64), fsm_t[:, 3:4], fsm_t[:, 4 + layer:5 + layer], ALU.mult, ALU.add,
                       extra_reads=FSM)
                    ts("dve", TB, TA, PI, -2.0 * PI, ALU.is_gt, ALU.mult)
                    tt("dve", TA, TA, TB, ALU.add)
                    ts("dve", TB, TA, -PI, 2.0 * PI, ALU.is_lt, ALU.mult)
                    tt("dve", TA, TA, TB, ALU.add)
                    if layer < 2:
                        act(Hb(d, 0, 2048), TA, AF.Sin)
                    else:
                        act(H16(d), TA, AF.Sin)
            bank["n"] = 0
            for tc in range(16):
                D1 = YH.f32(4096, 5120)
                D2 = YH.f32(5120, 6144)
                HF = YH.bf(12288, 13312)
                HB = YH.bf(13312, 14336)
                act(D1, S3.f32(0, 1024), AF.Exp, scale=tv_t[:, tc:tc + 1], extra_reads=[("tv",)])
                act(D2, S3.f32(0, 1024), AF.Exp, scale=tv_t[:, 16 + tc:17 + tc], extra_reads=[("tv",)])
                for d in range(2):
                    for half in range(2):
                        b = nb()
                        h16 = H16(d)
                        mm(PS(b), Reg(h16.ap[:, tc * 128:tc * 128 + 128], h16.keys),
                           W4(d * 1024 + half * 512, d * 1024 + half * 512 + 512), True, True)
                        dst = (HF if d == 0 else HB)
                        dstr = Reg(dst.ap[:, half * 512:half * 512 + 512], dst.keys)
                        dec = (D1 if d == 0 else D2)
                        decr = Reg(dec.ap[:, half * 512:half * 512 + 512], dec.keys)
                        tt("dve", dstr, PS(b), decr, ALU.mult)
                if tc == 0:
                    r0 = Reg(HF.ap[0:1, :], HF.keys)
                    tt("dve", r0, r0, SKIP, ALU.add)
                tt("dve", S1.bf(tc * 1024, tc * 1024 + 1024), HF, HB, ALU.add)
                tt("dve", YA.bf(tc * 1024, tc * 1024 + 1024), HF, HB, ALU.subtract)
            for X in (S1, YA):
                for dc in range(8):
                    for half in range(2):
                        hi = X.bf((8 + dc) * 1024 + half * 512, (8 + dc) * 1024 + half * 512 + 512)
                        lo = X.bf((7 - dc) * 1024 + half * 512, (7 - dc) * 1024 + half * 512 + 512)
                        be, bo = nb(), nb()
                        mm(PS(be), IDENT, hi, True, False)
                        mm(PS(be), JMAT, lo, False, True)
                        mm(PS(bo), IDENT, hi, True, False)
                        mm(PS(bo), NJMAT, lo, False, True)
                        act(hi, PS(be), AF.Identity)
                        act(lo, PS(bo), AF.Identity)
            order = []
            for j in range(16):
                order += [j, 16 + j]
            stg = [S3.bf(6144, 7168), S3.bf(7168, 8192)]
            fbs = {}
            for n in range(6):
                fbs[n] = load_F2(order[n])
            ROT = [("rot",)]
            for j in range(16):
                src = S1 if (j < 8) else YA
                pb4 = (j % 2) * 4
                for part in range(2):
                    n = 2 * j + part
                    fb = fbs[n]
                    for half in range(2):
                        for dc in range(8):
                            slot = (8 + dc) if part == 0 else (7 - dc)
                            mm(PS(pb4 + part * 2 + half), Fblk(fb, dc),
                               src.bf(slot * 1024 + half * 512, slot * 1024 + half * 512 + 512), dc == 0, dc == 7)
                    if n + 6 < 32:
                        fbs[n + 6] = load_F2(order[n + 6])
                t0c = 4096 + (j % 2) * 2048
                VR, VI = PSW(pb4, 2), PSW(pb4 + 2, 2)
                TA_ = YH.f32(t0c, t0c + 1024)
                TB_ = YH.f32(t0c + 1024, t0c + 2048)
                U1 = W32.f32(0, 1024)
                U2 = W32.f32(1024, 2048)
                ts("dve", TA_, VR, rot_t[:, j * 4 + 0:j * 4 + 1], None, ALU.mult, extra_reads=ROT)
                ts("dve", TB_, VI, rot_t[:, j * 4 + 2:j * 4 + 3], None, ALU.mult, extra_reads=ROT)
                ts("dve", U1, VI, rot_t[:, j * 4 + 1:j * 4 + 2], None, ALU.mult, extra_reads=ROT)
                ts("dve", U2, VR, rot_t[:, j * 4 + 3:j * 4 + 4], None, ALU.mult, extra_reads=ROT)
                tt("dve", stg[0], TA_, U1, ALU.add)
                tt("dve", stg[1], TB_, U2, ALU.add)
                dma("sp", ksp_d[j, 0], stg[0].ap, stg[0].keys, [("ksp", j, 0)])
                if j == 0:
                    dma("sp", ksp_d[j, 2], stg[0].ap, stg[0].keys, [("ksp", j, 2)])
                dma("sp", ksp_d[j, 1], stg[1].ap, stg[1].keys, [("ksp", j, 1)])
                if j == 0:
                    dma("sp", ksp_d[0, 2, 0:1, :], stg[1].ap[0:1, :], stg[1].keys, [("ksp", 0, 2)])
                    dma("sp", ksp_d[0, 1, 0:1, :], ZROW.ap, ZROW.keys, [("ksp", 0, 1)])

        def hyena_pass(seq, hh, widx0, next_idxs):
            ACC = [YA.bf(0, 2048), YA.bf(2048, 4096), YA.bf(4096, 6144)]
            HG = YA.bf(6144, 8192)
            P0 = YA.bf(8192, 10242)
            P2 = YA.bf(10244, 12294)
            ZTS = [YA.bf(12296, 14344), W32.bf(0, 2048)]
            ZTT = [(YA.t, 12296), (W32.t, 0)]
            ZBS = [W32.bf(2048, 4096), W32.bf(4096, 6144)]
            preload_w([widx0 + k for k in range(4)])
            memset("pool", Reg(P0.ap[:, 0:1], P0.keys), 0.0)
            memset("pool", Reg(P2.ap[:, 2049:2050], P2.keys), 0.0)
            kinds = [1, 2, 0]
            pbank = {"n": 0}

            def nb():
                b = pbank["n"] % 8
                pbank["n"] += 1
                return b

            def proj_chunk(cc):
                ch = hh * 4 + cc
                ZT = ZTS[cc % 2]
                for ki in range(4):
                    slot = load_w(widx0 + cc * 4 + ki)
                    for tq in range(4):
                        b = nb()
                        for dc in range(8):
                            mm(PS(b), wchunk(slot, dc), xTr(dc, tq * 512, tq * 512 + 512), dc == 0, dc == 7)
                        if ki == 3:
                            act(Reg(HG.ap[:, tq * 512:tq * 512 + 512], HG.keys[tq:tq + 1]), PS(b), AF.Silu)
                        else:
                            k = kinds[ki]
                            base = (k * 8 + ch) * 3
                            A = ACC[ki]
                            act(Reg(A.ap[:, tq * 512:tq * 512 + 512], A.keys[tq:tq + 1]), PS(b), AF.Identity,
                                scale=cw_t[:, base + 1:base + 2], bias=cb_t[:, k * 8 + ch:k * 8 + ch + 1],
                                extra_reads=[("cw",), ("cb",)])
                            act(Reg(P0.ap[:, 1 + tq * 512:1 + tq * 512 + 512], P0.keys), PS(b), AF.Identity,
                                scale=cw_t[:, base:base + 1], extra_reads=[("cw",)])
                            act(Reg(P2.ap[:, 1 + tq * 512:1 + tq * 512 + 512], P2.keys), PS(b), AF.Identity,
                                scale=cw_t[:, base + 2:base + 3], extra_reads=[("cw",)])
                    if ki < 3:
                        A = ACC[ki]
                        tt("dve", A, A, Reg(P0.ap[:, 0:2048], P0.keys), ALU.add)
                        tt("dve", A, A, Reg(P2.ap[:, 2:2050], P2.keys), ALU.add)
                    if ki == 1:
                        tt("dve", ZT, ACC[0], ACC[1], ALU.mult)
                        ZB = ZBS[cc % 2]
                        zt_t, zt_c0 = ZTT[cc % 2]
                        hi = Reg(zt_t[:, zt_c0 + 1024:zt_c0 + 2048], ZT.keys)
                        lo = Reg(rev(zt_t, zt_c0, 1024), ZT.keys)
                        tt("dve", Reg(ZB.ap[:, 1024:2048], ZB.keys), hi, lo, ALU.add)
                        tt("dve", Reg(ZB.ap[:, 0:1024], ZB.keys), hi, lo, ALU.subtract)
                tt("dve", YH.bf(ch * 2048, ch * 2048 + 2048), ACC[2], HG, ALU.mult)

            def transposes(cc):
                ZT = ZBS[cc % 2]
                for g4 in range(4):
                    b = nb()
                    for j in range(4):
                        s_c = g4 * 4 + j
                        mm(PS(b, 0, 128, j * 128, j * 128 + 128),
                           Reg(ZT.ap[:, s_c * 128:s_c * 128 + 128], ZT.keys), IDENT, True, True)
                    o = S2.bfv(g4 * 2048, g4 * 2048 + 2048, 512, 0, 4, cc * 128, cc * 128 + 128)
                    i = Reg(ps[b][:, :].rearrange("p (a b) -> p a b", b=128), [("ps", b)])
                    cp("act", o, i)

            for cc in range(4):
                proj_chunk(cc)
                if cc >= 1:
                    transposes(cc - 1)
            transposes(3)

            if DEBUG and seq == 0 and hh == 0:
                dma("sp", dbg_d[3, :, 0:8192], S2.t[:, :], S2.keys(0, 8192), [("dbg", 3)])
            def load_KT(jj):
                k0 = 6144 + (jj % 2) * 1536
                kt = [S3.bf(k0 + t * 512, k0 + t * 512 + 512) for t in range(3)]
                for t in range(3 if jj == 0 else 2):
                    dma("sp", kt[t].ap, ksp_d[jj, t, :, hh * 512:hh * 512 + 512], [("ksp", jj, t)], kt[t].keys)
                if jj != 0:
                    kt[2] = kt[0]
                return kt
            forder = []
            for j in range(16):
                forder += [j, 16 + j]
            KTS = {0: load_KT(0), 1: load_KT(1)}
            pre_g = {}
            fbs = {n: load_F2(forder[n]) for n in range(6)}
            for j in range(16):
                KT = KTS[j]
                pb = (j % 4) * 2
                for part in range(2):
                    n = 2 * j + part
                    fb = fbs[n]
                    for dc in range(8):
                        slot = (8 + dc) if part == 0 else dc
                        mm(PS(pb + part), Fblk(fb, dc), S2.bf(slot * 512, slot * 512 + 512), dc == 0, dc == 7)
                    if n + 6 < 32:
                        fbs[n + 6] = load_F2(forder[n + 6])
                w0 = (j % 2) * 2048
                T = [W32.bf(w0 + t * 512, w0 + t * 512 + 512) for t in range(4)]
                tt("dve", T[0], PS(pb), KT[0], ALU.mult)
                tt("dve", T[1], PS(pb + 1), KT[1], ALU.mult)
                tt("dve", T[2], PS(pb), KT[1], ALU.mult)
                tt("dve", T[3], PS(pb + 1), KT[2], ALU.mult)
                tt("dve", S1.bf(j * 512, j * 512 + 512), T[0], T[1], ALU.subtract)
                tt("dve", S1.bf((16 + j) * 512, (16 + j) * 512 + 512), T[2], T[3], ALU.add)
                if j + 2 < 16:
                    KTS[j + 2] = load_KT(j + 2)
                if j == 12:
                    for m_ in range(2):
                        r_ = W32.bf(4096 + m_ * 2048, 4096 + m_ * 2048 + 2048)
                        dma("sp", r_.ap.rearrange("p (a b) -> p a b", b=512),
                            G2_d[0, 0, 0][:, m_ * 4:m_ * 4 + 4, :], [], r_.keys)
                        pre_g[m_] = r_

            preload_w(next_idxs)
            gstate = 0
            epi = 0
            for h2 in range(2):
                for kind in range(2):
                    for g4 in range(4):
                        grp, q4 = g4 // 2, g4 % 2
                        if gstate in pre_g:
                            gr = pre_g[gstate]
                        else:
                            gb = (gstate - 2) % 4
                            gr = S2.bf(gb * 2048, gb * 2048 + 2048)
                            dma("sp", gr.ap.rearrange("p (a b) -> p a b", b=512),
                                G2_d[h2, kind, grp][:, q4 * 4:q4 * 4 + 4, :], [], gr.keys)
                        gstate += 1
                        for f4 in range(4):
                            fcl = q4 * 4 + f4
                            fc = kind * 16 + grp * 8 + fcl
                            for cc in range(4):
                                mm(PS(kind * 4 + cc), S1.bf(fc * 512 + cc * 128, fc * 512 + cc * 128 + 128),
                                   Reg(gr.ap[:, f4 * 512:f4 * 512 + 512], gr.keys),
                                   g4 == 0 and f4 == 0, g4 == 3 and f4 == 3)
                    if kind == 0:
                        for cc in range(4):
                            act(W32.f32(cc * 512, cc * 512 + 512), PS(cc), AF.Identity)
                for cc in range(4):
                    ch = hh * 4 + cc
                    YE = W32.f32(cc * 512, cc * 512 + 512)
                    w0 = 2048 + (epi % 2) * 1024
                    epi += 1
                    TP = W32.f32(w0, w0 + 512)
                    TM = W32.f32(w0 + 512, w0 + 1024)
                    tt("dve", TP, PS(4 + cc), YE, ALU.add)
                    tt("dve", TM, YE, PS(4 + cc), ALU.subtract)
                    rp = YH.bf(ch * 2048 + 1024 + h2 * 512, ch * 2048 + 1024 + h2 * 512 + 512)
                    tt("dve", rp, TP, rp, ALU.mult)
                    lo = ch * 2048 + 512 - h2 * 512
                    rm = YH.bf(lo, lo + 512)
                    tmr = Reg(rev(W32.t32, w0 + 512, 512), TM.keys)
                    tt("dve", rm, tmr, rm, ALU.mult)

        def attention_half(seq, gp, widx0, next_idxs):
            KZ = [S2.bf(0, 2048), S2.bf(2048, 4096)]
            VZ0 = 4096
            QT0 = 0
            AG0 = 8192
            BT0 = 0
            PT0 = 3072
            preload_w([widx0 + k for k in range(4)])
            memset("dve", S2.bf(0, 2048, 64, 128), 0.0)
            memset("dve", S2.bf(2048, 4096, 0, 64), 0.0)
            memset("dve", S2.bf(4096, 8192), 1.0)
            for e in range(2):
                for rel in range(3):
                    r = S3.bf(BT0 + (e * 3 + rel) * 512, BT0 + (e * 3 + rel) * 512 + 512)
                    dma("pool", r.ap, bt_d[gp, e, rel], [], r.keys)
            pbank = {"n": 0}

            def nb():
                b = pbank["n"] % 8
                pbank["n"] += 1
                return b
            slot = load_w(widx0 + 0)
            for tq in range(4):
                b = nb()
                for dc in range(8):
                    mm(PS(b), wchunk(slot, dc), xTr(dc, tq * 512, tq * 512 + 512), dc == 0, dc == 7)
                act(S2.bf(tq * 512, tq * 512 + 512, 0, 64), PS(b, 0, 64), AF.Identity)
                act(S2.bf(2048 + tq * 512, 2048 + tq * 512 + 512, 64, 128), PS(b, 64, 128), AF.Identity)
            slot = load_w(widx0 + 1)
            for g4 in range(4):
                b = nb()
                for j in range(4):
                    tk = g4 * 4 + j
                    for dc in range(8):
                        mm(PS(b, 0, 128, j * 128, j * 128 + 128), xTr(dc, tk * 128, tk * 128 + 128),
                           wchunk(slot, dc), dc == 0, dc == 7)
                c0 = VZ0 + g4 * 1024
                pv = ps[b][:, :].rearrange("p (a b) -> p a b", b=128)
                o0 = S2.bfv(c0, c0 + 1024, 256, 0, 4, 0, 64)
                o1 = S2.bfv(c0, c0 + 1024, 256, 0, 4, 192, 256)
                cp("act", o0, Reg(pv[:, :, 0:64], [("ps", b)]))
                cp("act", o1, Reg(pv[:, :, 64:128], [("ps", b)]))
            for i in range(4):
                slot = load_w(widx0 + 2 + i)
                for tq in range(4):
                    b = nb()
                    for dc in range(8):
                        mm(PS(b), wchunk(slot, dc), xTr(dc, tq * 512, tq * 512 + 512), dc == 0, dc == 7)
                    c = QT0 + i * 2048 + tq * 512
                    act(S1.bf(c, c + 512), PS(b), AF.Identity, scale=0.125)
            for i in range(4):
                slot = load_w(widx0 + 6 + i)
                for tq in range(4):
                    b = nb()
                    for dc in range(8):
                        mm(PS(b), wchunk(slot, dc), xTr(dc, tq * 512, tq * 512 + 512), dc == 0, dc == 7)
                    c = AG0 + i * 2048 + tq * 512
                    act(S1.bf(c, c + 512), PS(b), AF.Silu)

            ptn = {"n": 0}
            sbn = {"n": 0}

            def qk_exp(qb):
                tiles = []
                for e in range(2):
                    for rel in (-1, 0, 1):
                        kb = qb + rel
                        if kb < 0 or kb > 15:
                            continue
                        b = sbn["n"] % 4
                        sbn["n"] += 1
                        qv = S1.bfv(QT0, QT0 + 8192, 2048, 0, 4, qb * 128, qb * 128 + 128)
                        mm(PS(b), Reg(KZ[e].ap[:, kb * 128:kb * 128 + 128], KZ[e].keys), qv, True, False)
                        br = S3.bf(BT0 + (e * 3 + rel + 1) * 512, BT0 + (e * 3 + rel + 1) * 512 + 512)
                        mm(PS(b), IDENT, br, False, True)
                        pslot = ptn["n"] % 12
                        ptn["n"] += 1
                        pt = S3.bf(PT0 + pslot * 512, PT0 + pslot * 512 + 512)
                        act(pt, PS(b), AF.Exp)
                        tiles.append((e, kb, pt))
                return tiles

            def pv_norm(qb, tiles):
                tb = 4 + (qb % 2) * 2
                for e in range(2):
                    te = [t for t in tiles if t[0] == e]
                    for idx, (_, kb, pt) in enumerate(te):
                        vz = S2.bf(VZ0 + kb * 256 + e * 128, VZ0 + kb * 256 + e * 128 + 128)
                        mm(PS(tb + e), vz, pt, idx == 0, idx == len(te) - 1)
                w0 = (qb % 2) * 1024
                T0 = W32.f32(w0, w0 + 512)
                T1 = W32.f32(w0 + 512, w0 + 1024)
                ESK = [("es",)]
                c0 = gp * 512
                tt("dve", T0.rows(0, 64), PS(tb, 64, 128), Reg(es_t[64:128, c0:c0 + 512], ESK), ALU.add)
                tt("dve", T0.rows(64, 128), PS(tb + 1, 0, 64), Reg(es_t[0:64, c0:c0 + 512], ESK), ALU.add)
                act(T0, T0, AF.Ln)
                act(T0, T0, AF.Exp, scale=-1.0)
                tt("dve", T1.rows(0, 64), PS(tb, 0, 64), T0.rows(0, 64), ALU.mult)
                tt("dve", T1.rows(64, 128), PS(tb + 1, 64, 128), T0.rows(64, 128), ALU.mult)
                o = YA.bfv(gp * 8192, gp * 8192 + 8192, 2048, 0, 4, qb * 128, qb * 128 + 128)
                ag = S1.bfv(AG0, AG0 + 8192, 2048, 0, 4, qb * 128, qb * 128 + 128)
                t1v = Reg(W32.t32[:, w0 + 512:w0 + 1024].rearrange("p (a b) -> p a b", b=128), T1.keys)
                tt("dve", o, t1v, ag, ALU.mult)

            prev = None
            for qb in range(16):
                if qb == 11:
                    preload_w(next_idxs)
                tiles = qk_exp(qb)
                if prev is not None:
                    pv_norm(qb - 1, prev)
                prev = tiles
            pv_norm(15, prev)

        def tail(seq, widx0, after_part1, mid_part2):
            def load_tw(idx, slot):
                r = S3.bf(slot * 1024, slot * 1024 + 1024)
                dma("pool", r.ap.rearrange("p (a b) -> p a b", b=128), wst_d[idx], [], r.keys)

            def tw(slot, dc):
                if isinstance(slot, tuple):
                    return wchunk(slot[1], dc)
                return S3.bf(slot * 1024 + dc * 128, slot * 1024 + dc * 128 + 128)

            first = [("wb", load_w(widx0 + k)) for k in range(4)]
            wr = S2.bf(0, 8192)
            dma("pool", wr.ap.rearrange("p (a b) -> p a b", b=1024), wout_d, [], wr.keys)
            pb = 0
            for oc in range(8):
                if oc + 1 < 8:
                    for k in range(4):
                        load_tw(widx0 + (oc + 1) * 4 + k, ((oc + 1) % 2) * 4 + k)
                slots = first if oc == 0 else [(oc % 2) * 4 + k for k in range(4)]
                for tq in range(4):
                    base = (pb % 2) * 4
                    w0 = (pb % 2) * 2048
                    pb += 1
                    for k in range(4):
                        for dc in range(8):
                            if k < 2:
                                rhs = xTr(dc, tq * 512, tq * 512 + 512)
                            elif k == 2:
                                rhs = YA.bf(dc * 2048 + tq * 512, dc * 2048 + tq * 512 + 512)
                            else:
                                rhs = YH.bf(dc * 2048 + tq * 512, dc * 2048 + tq * 512 + 512)
                            mm(PS(base + k), tw(slots[k], dc), rhs, dc == 0, dc == 7)
                    GA = W32.f32(w0, w0 + 512)
                    GH = W32.f32(w0 + 512, w0 + 1024)
                    M1 = W32.f32(w0 + 1024, w0 + 1536)
                    M2 = W32.f32(w0 + 1536, w0 + 2048)
                    act(GA, PS(base + 0), AF.Sigmoid)
                    act(GH, PS(base + 1), AF.Sigmoid)
                    tt("dve", M1, PS(base + 2), GA, ALU.mult)
                    tt("dve", M2, PS(base + 3), GH, ALU.mult)
                    tt("pool", S1.bf(oc * 2048 + tq * 512, oc * 2048 + tq * 512 + 512), M1, M2, ALU.add)
            if DEBUG and seq == 0:
                dma("sp", dbg_d[2], S1.t[:, :], S1.keys(0, 16384), [("dbg", 2)])
            def XRb(tk):
                k = tk % 4
                return YH.f32(k * 1024, k * 1024 + 1024)

            def Rb(tk):
                k = tk % 4
                return YH.f32(4096 + k * 1024, 4096 + k * 1024 + 1024)

            def load_xr(tk):
                row0 = seq * S + tk * 128
                dma("sp", XRb(tk).ap, xr_d[row0:row0 + 128, :], [], XRb(tk).keys)

            for tk in range(3):
                load_xr(tk)
            after_part1()

            def stage1a(tk):
                base = (tk % 4) * 2
                XR = XRb(tk)
                for half in range(2):
                    for dc in range(8):
                        mm(PS(base + half), S1.bf(dc * 2048 + tk * 128, dc * 2048 + tk * 128 + 128),
                           S2.bf(dc * 1024 + half * 512, dc * 1024 + half * 512 + 512), dc == 0, False)
                    xs = Reg(XR.ap[:, half * 512:half * 512 + 512], XR.keys)
                    mm(PS(base + half), AID, xs, False, True)
                so = (tk % 4) * 16
                ST = [("st", tk % 4)]
                PR = PSW(base, 2)
                JK = Rb(tk)
                sc.add("act", lambda e_, o=JK.ap, i=PR.ap, a=st_t[:, so + 0:so + 1]:
                       e_.activation(out=o, in_=i, func=AF.Identity, accum_out=a),
                       reads=PR.keys, writes=JK.keys + ST)
                sc.add("act", lambda e_, o=JK.ap, i=PR.ap, a=st_t[:, so + 1:so + 2]:
                       e_.activation(out=o, in_=i, func=AF.Square, accum_out=a),
                       reads=PR.keys, writes=JK.keys + ST)

            def stage1b(tk):
                so = (tk % 4) * 16
                ST = [("st", tk % 4)]
                STR = lambda c: Reg(st_t[:, so + c:so + c + 1], ST)
                ts("dve", STR(12), STR(0), 1.0 / D, None, ALU.mult)
                tt("dve", STR(3), STR(12), STR(12), ALU.mult)
                stt(STR(13), STR(1), 1.0 / D, STR(3), ALU.mult, ALU.subtract)
                ts("dve", STR(13), STR(13), LN_EPS, None, ALU.add)
                act(STR(14), STR(13), AF.Sqrt)
                sc.add("dve", lambda e_, o=st_t[:, so + 14:so + 15]: e_.reciprocal(out=o, in_=o), reads=ST, writes=ST)
                ts("dve", STR(15), STR(12), st_t[:, so + 14:so + 15], -1.0, ALU.mult, ALU.mult)

            def stage2(tk):
                base = (tk % 4) * 2
                R = Rb(tk)
                so = (tk % 4) * 16
                ST = [("st", tk % 4)]
                row0 = seq * S + tk * 128
                act(R, PSW(base, 2), AF.Identity, scale=st_t[:, so + 14:so + 15], bias=st_t[:, so + 15:so + 16],
                    extra_reads=ST)
                tt("dve", R, R, Reg(lng_t[:], [("lng",)]), ALU.mult)
                tt("dve", R, R, Reg(lnb_t[:], [("lnb",)]), ALU.add)
                dma("pool", out_d[row0:row0 + 128, :], R.ap, R.keys, [("out", row0)])

            for tk in range(16):
                if tk + 3 < 16:
                    load_xr(tk + 3)
                stage1a(tk)
                if tk >= 1:
                    stage1b(tk - 1)
                if tk >= 2:
                    stage2(tk - 2)
                if tk == 8:
                    mid_part2()
            stage1b(15)
            stage2(14)
            stage2(15)

        def load_xT(seq, dcs=range(8)):
            for dc in dcs:
                r = XT.bf(dc * 2048, dc * 2048 + 2048)
                dma("pool", r.ap, xT_d[seq, :, dc, :], [], r.keys)

        def load_xT_split_a(seq):
            for dc in range(4):
                r = YA.f32(dc * 2048, dc * 2048 + 2048)
                dma("sp", r.ap, xT_d[seq, :, dc, :], [], r.keys)
            load_xT(seq, range(4, 8))

        def load_xT_split_b(seq):
            for dc in range(4):
                cp("dve", XT.bf(dc * 2048, dc * 2048 + 2048), YA.f32(dc * 2048, dc * 2048 + 2048))

        filter_phase()
        load_xT(0)
        for seq in range(NSEQ):
            hyena_pass(seq, 0, 0, [16, 17, 18, 19])
            hyena_pass(seq, 1, 16, [32, 33, 34, 35])
            if DEBUG and seq == 0:
                dma("sp", dbg_d[0], YH.t[:, :], YH.keys(0, 16384), [("dbg", 0)])
            attention_half(seq, 0, 32, [42, 43, 44, 45])
            attention_half(seq, 1, 42, [52, 53, 54, 55])
            if DEBUG and seq == 0:
                dma("sp", dbg_d[1], YA.t[:, :], YA.keys(0, 16384), [("dbg", 1)])
            tail(seq, 52, (lambda s_=seq: (load_xT_split_a(s_ + 1), preload_w([0, 1, 2, 3]))) if seq + 1 < NSEQ else (lambda: None),
                 (lambda s_=seq: load_xT_split_b(s_ + 1)) if seq + 1 < NSEQ else (lambda: None))

        sems = {e: es.enter_context(nc.semaphore(f"sem_{e}")) for e in ("pe", "act", "dve", "pool")}
        dma_sems = {
            "sp": [es.enter_context(nc.semaphore(f"dsp{i}")) for i in range(24)],
            "pool": [es.enter_context(nc.semaphore(f"dpl{i}")) for i in range(12)],
            "act": [], "pe": [], "dve": [],
        }
        block = es.enter_context(nc.Block())
        sc.emit(nc, block, sems, dma_sems)
    return nc


def _t5_bucket(rel):
    half = 16
    max_exact = 8
    ret = (rel > 0).astype(np.int32) * half
    n = np.abs(rel)
    n_safe = np.maximum(n, 1).astype(np.float32)
    large = max_exact + (np.log(n_safe / max_exact) / math.log(128 / max_exact) * (half - max_exact)).astype(np.int32)
    large = np.minimum(large, half - 1)
    return (ret + np.where(n < max_exact, n, large)).astype(np.int32)


def _wchunk(W, cols):
    return np.ascontiguousarray(W[:, cols].reshape(8, 128, 128).transpose(1, 0, 2))


_CONST = {}


def _constants():
    if _CONST:
        return _CONST
    N = 2 * S
    perm = np.concatenate([np.arange(0, S, 2), np.arange(1, S, 2)]).astype(np.float64)
    s = np.arange(S, dtype=np.float64)[:, None]
    ang = 2 * np.pi * perm[None, :] * s / N
    F = np.concatenate([np.cos(ang), -np.sin(ang)], axis=1)
    F[:, S] = (-1.0) ** np.arange(S)
    G = np.concatenate([2.0 / N * np.cos(ang.T), -2.0 / N * np.sin(ang.T)], axis=0)
    G[0, :] = 1.0 / N
    G[S, :] = (1.0 / N) * (-1.0) ** np.arange(S)
    H = S // 2
    theta = 2 * np.pi * perm / N
    jj = np.arange(H, dtype=np.float64) + 0.5
    Fc = np.cos(jj[:, None] * theta[None, :])
    Fs = -np.sin(jj[:, None] * theta[None, :])
    Fs[:, 0] = -np.sin(np.pi * jj)
    F2 = np.concatenate([Fc, Fs], axis=1)
    F2h = F2.reshape(8, 128, 32, 128).transpose(2, 1, 0, 3)
    _CONST["F2"] = np.ascontiguousarray(F2h).astype(ml_dtypes.bfloat16)
    Gc = (2.0 / N) * np.cos(theta[:, None] * jj[None, :])
    Gc[0, :] = 1.0 / N
    Gs = -(2.0 / N) * np.sin(theta[:, None] * jj[None, :])
    Gs[0, :] = -(1.0 / N) * np.sin(np.pi * jj)
    G2 = np.stack([Gc, Gs], axis=0)
    G2h = G2.reshape(2, 2, 8, 128, 2, 512).transpose(4, 0, 1, 3, 2, 5)
    _CONST["G2"] = np.ascontiguousarray(G2h).astype(ml_dtypes.bfloat16)
    Jm = np.eye(128, dtype=np.float32)[::-1]
    _CONST["jm"] = np.ascontiguousarray(np.concatenate([Jm, -Jm], axis=1)).astype(ml_dtypes.bfloat16)
    phi = np.mod(theta * 1023.5, 2 * np.pi)
    cr, sr, ci, nsi = np.cos(phi), np.sin(phi), np.cos(phi).copy(), -np.sin(phi)
    cr[0], sr[0], ci[0], nsi[0] = 1.0, 0.0, -1.0, 0.0
    rot = np.stack([cr, sr, ci, nsi], axis=1).reshape(16, 128, 4).transpose(1, 0, 2).reshape(128, 64)
    _CONST["rot"] = np.ascontiguousarray(rot).astype(np.float32)
    _CONST["ident"] = np.eye(128, dtype=np.float32).astype(ml_dtypes.bfloat16)
    _CONST["aid"] = (np.eye(128, dtype=np.float32) * np.float32(ALPHA)).astype(np.float32)
    f32 = np.float32
    bands = 16
    t = np.linspace(0.0, 1.0, S, dtype=f32)[:, None]
    w = (2.0 * math.pi * np.arange(S, dtype=f32)[:, None] / S).astype(f32)
    fb = np.linspace(1e-4, bands - 1, bands, dtype=f32)[None, :]
    z = np.concatenate([t, np.cos(fb * w), -np.sin(fb * w)], axis=-1).astype(f32)
    zr = np.empty_like(z)
    zr[1:] = z[:0:-1]
    zr[0] = z[0]
    _CONST["fz"] = np.ascontiguousarray(np.concatenate([z.T, zr.T], axis=1)).astype(f32)
    max_decay = math.log(1e-2) / 0.3
    min_decay = math.log(1e-2) / 1.5
    deltas = np.abs(np.linspace(min_decay, max_decay, 1024, dtype=f32))
    _CONST["dl"] = np.ascontiguousarray(np.broadcast_to(deltas[None, :], (128, 1024))).astype(f32)
    tvals = t[:, 0]
    tv = np.zeros((128, 32), f32)
    tv[:, 0:16] = -tvals.reshape(16, 128).T
    trev = np.empty(S, f32)
    trev[1:] = tvals[:0:-1]
    trev[0] = 1e4
    tv[:, 16:32] = -trev.reshape(16, 128).T
    _CONST["tv"] = tv
    key = np.arange(128)[:, None]
    q = np.arange(128)[None, :]
    geo = []
    for rel in (-1, 0, 1):
        rp = rel * 128 + key - q
        geo.append((_t5_bucket(rp), np.abs(rp) <= 128))
    _CONST["geo"] = geo
    return _CONST


def _feat_order():
    order = []
    for gp in range(2):
        for i in range(4):
            h0 = 4 * (2 * gp) + i
            h1 = 4 * (2 * gp + 1) + i
            order.append(np.concatenate([h0 * 64 + np.arange(64), h1 * 64 + np.arange(64)]))
    return order


def _prep_shared(inp):
    C = _constants()
    f32 = np.float32
    w_in = np.asarray(inp["w_in"], f32)[0]
    Wa = np.asarray(inp["w_branch_attn"], f32)[0]
    Wh = np.asarray(inp["w_branch_hyena"], f32)[0]
    Wo = np.asarray(inp["w_out"], f32)[0]
    forder = _feat_order()
    chunks = []
    ar = np.arange(128)
    for hh in range(2):
        for cc in range(4):
            ch = hh * 4 + cc
            chunks.append(_wchunk(w_in, 2560 + 1024 + ch * 128 + ar))
            chunks.append(_wchunk(w_in, 2560 + 2048 + ch * 128 + ar))
            chunks.append(_wchunk(w_in, 2560 + ch * 128 + ar))
            chunks.append(_wchunk(w_in, 5632 + ch * 128 + ar))
    for gp in range(2):
        chunks.append(_wchunk(w_in, 1024 + gp * 128 + ar))
        chunks.append(_wchunk(w_in, 1280 + gp * 128 + ar))
        for i in range(4):
            chunks.append(_wchunk(w_in, forder[gp * 4 + i]))
        for i in range(4):
            chunks.append(_wchunk(w_in, 1536 + forder[gp * 4 + i]))
    Wa_p = Wa[np.concatenate(forder), :]
    for oc in range(8):
        chunks.append(_wchunk(w_in, 6656 + oc * 128 + ar))
        chunks.append(_wchunk(w_in, 7680 + oc * 128 + ar))
        chunks.append(_wchunk(Wa_p, oc * 128 + ar))
        chunks.append(_wchunk(Wh, oc * 128 + ar))
    wst = np.ascontiguousarray(np.stack(chunks, axis=0))
    assert wst.shape == (NW, 128, 8, 128)
    wout = np.ascontiguousarray(Wo.reshape(8, 128, 1024).transpose(1, 0, 2))
    conv_w = np.asarray(inp["conv_w"], f32)[0]
    conv_b = np.asarray(inp["conv_b"], f32)[0]
    cw = np.ascontiguousarray(conv_w.reshape(3, 3, 8, 128).transpose(3, 1, 2, 0).reshape(128, 72))
    cb = np.ascontiguousarray(conv_b.reshape(3, 8, 128).transpose(2, 0, 1).reshape(128, 24))
    sink = np.asarray(inp["attn_sink"], f32)[0]
    rel_bias = np.asarray(inp["rel_bias"], f32)
    esin = np.zeros((128, 2, 4, 128), f32)
    bt = np.full((2, 2, 3, 128, 4, 128), NEG, f32)
    for gp in range(2):
        for e in range(2):
            for i in range(4):
                h = 4 * (2 * gp + e) + i
                esin[(1 - e) * 64:(2 - e) * 64, gp, i, :] = sink[h]
                for r in range(3):
                    bucket, valid = C["geo"][r]
                    bt[gp, e, r, :, i, :] = np.where(valid, rel_bias[bucket, h], f32(NEG))
    shared = {
        "wst": wst, "wout": wout, "cw": cw, "cb": cb,
        "esin": np.ascontiguousarray(esin.reshape(128, 1024)),
        "bt": np.ascontiguousarray(bt.reshape(2, 2, 3, 128, 512)),
        "lng": np.ascontiguousarray(np.broadcast_to(np.asarray(inp["ln_g"], f32)[0][None, :], (128, D))),
        "lnb": np.ascontiguousarray(np.broadcast_to(np.asarray(inp["ln_b"], f32)[0][None, :], (128, D))),
        "fz": C["fz"], "dl": C["dl"], "tv": C["tv"],
        "F2": C["F2"], "G2": C["G2"], "ident": C["ident"], "aid": C["aid"], "jm": C["jm"], "rot": C["rot"],
        "skip": np.ascontiguousarray(np.asarray(inp["hyena_skip"], f32)[0][None, :]),
    }
    fwp = np.zeros((64, 2240), f32)
    fwp[:, 0:2048] = np.asarray(inp["filt_w4"], f32)[0]
    fwp[0:33, 2048:2112] = np.asarray(inp["filt_w1"], f32)[0]
    fwp[:, 2112:2176] = np.asarray(inp["filt_w2"], f32)[0]
    fwp[:, 2176:2240] = np.asarray(inp["filt_w3"], f32)[0]
    shared["fwp"] = fwp
    shared["fsm"] = np.ascontiguousarray(np.stack([
        np.asarray(inp["filt_b1"], f32)[0], np.asarray(inp["filt_b2"], f32)[0],
        np.asarray(inp["filt_b3"], f32)[0], np.asarray(inp["filt_freq"], f32)[0]], axis=1))
    return shared


_PROG = {}


def kernel(**inputs):
    x = np.asarray(inputs["x"], np.float32)
    shared = _prep_shared(inputs)
    if "nc" not in _PROG:
        _PROG["nc"] = build_program()
    nc = _PROG["nc"]
    in_maps = []
    for c in range(NCORE):
        xc = x[2 * c:2 * c + 2]
        xT = np.ascontiguousarray(xc.transpose(0, 2, 1).reshape(2, 8, 128, S).transpose(0, 2, 1, 3))
        m = dict(shared)
        m["xT"] = xT
        m["xr"] = np.ascontiguousarray(xc.reshape(2 * S, D))
        in_maps.append(m)
    res = run_bass_kernel_spmd(nc, in_maps, core_ids=list(range(NCORE)))
    out = np.stack([np.asarray(r["out"], np.float32).reshape(2, S, D) for r in res.results], axis=0)
    return out.reshape(16, S, D)
```

```python
import math
import contextlib
import numpy as np
import ml_dtypes
import concourse.bass as bass
import concourse.mybir as mybir
from concourse.bass_utils import run_bass_kernel_spmd

F32 = mybir.dt.float32
BF16 = mybir.dt.bfloat16
AF = mybir.ActivationFunctionType
ALU = mybir.AluOpType

D = 1024
S = 2048
NCORE = 8
NSEQ = 2
ALPHA = 2.0 ** 0.25
LN_EPS = 1e-5
NEG = -30000.0
PI = math.pi
NW = 84
DEBUG = False
NWB = 5


class Op:
    __slots__ = ("eng", "fn", "deps", "sig", "sigval", "is_dma", "dsem", "dval", "gi")

    def __init__(self, eng, fn, is_dma, gi):
        self.eng = eng
        self.fn = fn
        self.is_dma = is_dma
        self.deps = ()
        self.sig = False
        self.sigval = 0
        self.dsem = None
        self.dval = 0
        self.gi = gi


class Reg:
    __slots__ = ("ap", "keys")

    def __init__(self, ap, keys):
        self.ap = ap
        self.keys = list(keys)

    def rows(self, p0, p1):
        return Reg(self.ap[p0:p1], self.keys)


class Sched:
    STREAMS = ("pe", "act", "dve", "pool", "sp")

    def __init__(self):
        self.ops = {e: [] for e in self.STREAMS}
        self.last_w = {}
        self.readers = {}
        self.n = 0

    def add(self, eng, fn, reads=(), writes=(), dma=False):
        op = Op(eng, fn, dma, self.n)
        self.n += 1
        deps = set()
        for k in reads:
            w = self.last_w.get(k)
            if w is not None:
                deps.add(w)
        for k in writes:
            w = self.last_w.get(k)
            if w is not None:
                deps.add(w)
            rd = self.readers.get(k)
            if rd:
                deps.update(rd.values())
        if eng == "pe" and not dma:
            deps = {d for d in deps if d.is_dma or d.eng != "pe"}
        op.deps = tuple(deps)
        for d in deps:
            d.sig = True
        for k in writes:
            self.last_w[k] = op
            self.readers[k] = {}
        for k in reads:
            rd = self.readers.setdefault(k, {})
            rk = ("dma", op.gi) if dma else eng
            rd[rk] = op
        self.ops[eng].append(op)
        return op

    def emit(self, nc, block, sems, dma_sems):
        for e in self.STREAMS:
            c = 0
            for op in self.ops[e]:
                if not op.is_dma and op.sig:
                    c += 1
                    op.sigval = c
        for e in self.STREAMS:
            pool = dma_sems.get(e)
            i = 0
            for op in self.ops[e]:
                if op.is_dma:
                    op.dsem = pool[i % len(pool)]
                    op.dval = 16 * (i // len(pool) + 1)
                    i += 1

        def run_stream(e, eng):
            waited = {}

            def wait(sem, val):
                if waited.get(sem.num if hasattr(sem, "num") else id(sem), 0) < val:
                    eng.wait_ge(sem, val)
                    waited[sem.num if hasattr(sem, "num") else id(sem)] = val

            for op in self.ops[e]:
                for d in op.deps:
                    if d.is_dma:
                        wait(d.dsem, d.dval)
                    else:
                        wait(sems[d.eng], d.sigval)
                if op.is_dma:
                    if op.dval > 16:
                        wait(op.dsem, op.dval - 16)
                    op.fn(eng).then_inc(op.dsem, 16)
                else:
                    ins = op.fn(eng)
                    if op.sig:
                        ins.then_inc(sems[e], 1)
            last = {}
            for op in self.ops[e]:
                if op.is_dma:
                    last[id(op.dsem)] = (op.dsem, op.dval)
            for sem, val in last.values():
                wait(sem, val)

        @block.tensor
        def _(eng):
            run_stream("pe", eng)

        @block.scalar
        def _(eng):
            run_stream("act", eng)

        @block.vector
        def _(eng):
            run_stream("dve", eng)

        @block.gpsimd
        def _(eng):
            run_stream("pool", eng)

        @block.sync
        def _(eng):
            run_stream("sp", eng)


class Arena:
    def __init__(self, name, t, ncols):
        self.name = name
        self.t = t
        self.t32 = t[:].bitcast(F32)
        self.ncols = ncols

    def keys(self, c0, c1):
        return [(self.name, u) for u in range(c0 // 512, (c1 - 1) // 512 + 1)]

    def bf(self, c0, c1, p0=0, p1=128):
        return Reg(self.t[p0:p1, c0:c1], self.keys(c0, c1))

    def f32(self, c0, c1, p0=0, p1=128):
        return Reg(self.t32[p0:p1, c0:c1], self.keys(2 * c0, 2 * c1))

    def bfv(self, c0, c1, inner, a0, a1, b0, b1, p0=0, p1=128):
        v = self.t[p0:p1, c0:c1].rearrange("p (a b) -> p a b", b=inner)[:, a0:a1, b0:b1]
        return Reg(v, self.keys(c0, c1))


def build_program():
    nc = bass.Bass("TRN2", target_bir_lowering=False)
    dt = nc.dram_tensor

    def din(name, shape, dtype=F32):
        return dt(name, list(shape), dtype, kind="ExternalInput").ap()

    xT_d = din("xT", [NSEQ, 128, 8, S])
    xr_d = din("xr", [NSEQ * S, D])
    wst_d = din("wst", [NW, 128, 8, 128])
    wout_d = din("wout", [128, 8, D])
    cw_d = din("cw", [128, 72])
    cb_d = din("cb", [128, 24])
    esin_d = din("esin", [128, 1024])
    bt_d = din("bt", [2, 2, 3, 128, 512])
    lng_d = din("lng", [128, D])
    lnb_d = din("lnb", [128, D])
    fz_d = din("fz", [33, 4096])
    fwp_d = din("fwp", [64, 2240])
    fsm_d = din("fsm", [64, 4])
    dl_d = din("dl", [128, 1024])
    tv_d = din("tv", [128, 32])
    skip_d = din("skip", [1, 1024])
    F2_d = din("F2", [32, 128, 8, 128], BF16)
    G2_d = din("G2", [2, 2, 2, 128, 8, 512], BF16)
    ident_d = din("ident", [128, 128], BF16)
    aid_d = din("aid", [128, 128])
    jm_d = din("jm", [128, 256], BF16)
    rot_d = din("rot", [128, 64])
    out_d = dt("out", [NSEQ * S, D], F32, kind="ExternalOutput").ap()
    ksp_d = dt("kspec", [16, 3, 128, 1024], BF16, kind="Internal").ap()
    dbg_d = dt("dbg", [4, 128, 16384], BF16, kind="ExternalOutput").ap() if DEBUG else None

    sc = Sched()
    es = contextlib.ExitStack()
    with es:
        def sb(name, shape, dtype):
            return es.enter_context(nc.sbuf_tensor(name, list(shape), dtype))

        XT = Arena("xT", sb("xT_s", [128, 16384], BF16), 16384)
        YH = Arena("yh", sb("yh_s", [128, 16384], BF16), 16384)
        YA = Arena("ya", sb("ya_s", [128, 16384], BF16), 16384)
        S1 = Arena("S1", sb("S1_s", [128, 16384], BF16), 16384)
        S2 = Arena("S2", sb("S2_s", [128, 8192], BF16), 8192)
        S3 = Arena("S3", sb("S3_s", [128, 9216], BF16), 9216)
        W32 = Arena("W32", sb("W32_s", [128, 8192], BF16), 8192)
        WB = Arena("wb", sb("wb_s", [128, NWB * 1024], BF16), NWB * 1024)
        es_t = sb("es_s", [128, 1024], F32)
        lng_t = sb("lng_s", [128, D], F32)
        lnb_t = sb("lnb_s", [128, D], F32)
        cw_t = sb("cw_s", [128, 72], F32)
        cb_t = sb("cb_s", [128, 24], F32)
        ident_t = sb("ident_s", [128, 128], BF16)
        ones_t = sb("ones_s", [128, 256], BF16)
        fsm_t = sb("fsm_s", [64, 8], F32)
        tv_t = sb("tv_s", [128, 32], F32)
        st_t = sb("st_s", [128, 64], F32)
        ps_all = es.enter_context(nc.psum_tensor("ps_all", [128, 4096], F32))
        ps = [ps_all[:, b * 512:(b + 1) * 512] for b in range(8)]

        def PS(b, p0=0, p1=128, c0=0, c1=512):
            return Reg(ps_all[p0:p1, b * 512 + c0:b * 512 + c1], [("ps", b)])

        def PSW(b0, nb_, p0=0, p1=128):
            return Reg(ps_all[p0:p1, b0 * 512:(b0 + nb_) * 512], [("ps", b) for b in range(b0, b0 + nb_)])

        def small(t, name, c0, c1, p0=0, p1=None):
            p1 = t.shape[0] if p1 is None else p1
            return Reg(t[p0:p1, c0:c1], [(name,)])

        def mm(out, lhsT, rhs, start, stop):
            sc.add("pe", lambda e, o=out.ap, l=lhsT.ap, r=rhs.ap, a=start, b=stop:
                   e.matmul(o, lhsT=l, rhs=r, start=a, stop=b),
                   reads=lhsT.keys + rhs.keys, writes=out.keys)

        def act(out, in_, func, scale=1.0, bias=0.0, extra_reads=()):
            sc.add("act", lambda e, o=out.ap, i=in_.ap, f=func, s=scale, b=bias:
                   e.activation(out=o, in_=i, func=f, bias=b, scale=s),
                   reads=in_.keys + list(extra_reads), writes=out.keys)

        def tt(eng, out, in0, in1, op):
            sc.add(eng, lambda e, o=out.ap, a=in0.ap, b=in1.ap, p=op:
                   e.tensor_tensor(out=o, in0=a, in1=b, op=p),
                   reads=in0.keys + in1.keys, writes=out.keys)

        def ts(eng, out, in0, s1, s2, op0, op1=None, extra_reads=()):
            if op1 is None:
                sc.add(eng, lambda e, o=out.ap, a=in0.ap, x=s1, p0=op0:
                       e.tensor_scalar(out=o, in0=a, scalar1=x, scalar2=None, op0=p0),
                       reads=in0.keys + list(extra_reads), writes=out.keys)
            else:
                sc.add(eng, lambda e, o=out.ap, a=in0.ap, x=s1, y=s2, p0=op0, p1=op1:
                       e.tensor_scalar(out=o, in0=a, scalar1=x, scalar2=y, op0=p0, op1=p1),
                       reads=in0.keys + list(extra_reads), writes=out.keys)

        def stt(out, in0, scalar, in1, op0, op1, extra_reads=()):
            sc.add("dve", lambda e, o=out.ap, a=in0.ap, s=scalar, b=in1.ap, p0=op0, p1=op1:
                   e.scalar_tensor_tensor(out=o, in0=a, scalar=s, in1=b, op0=p0, op1=p1),
                   reads=in0.keys + in1.keys + list(extra_reads), writes=out.keys)

        def cp(eng, out, in_):
            if eng == "act":
                act(out, in_, AF.Identity)
                return
            sc.add(eng, lambda e, o=out.ap, i=in_.ap: e.tensor_copy(out=o, in_=i),
                   reads=in_.keys, writes=out.keys)

        def memset(eng, out, val):
            sc.add(eng, lambda e, o=out.ap, v=val: e.memset(o, v), reads=(), writes=out.keys)

        def dma(q, out_ap, in_ap, reads, writes, **kw):
            sc.add(q, lambda e, o=out_ap, i=in_ap, k=kw: e.dma_start(out=o, in_=i, **k),
                   reads=reads, writes=writes, dma=True)

        dma("sp", cw_t[:], cw_d, [], [("cw",)])
        dma("sp", cb_t[:], cb_d, [], [("cb",)])
        dma("sp", ident_t[:], ident_d, [], [("ident",)])
        dma("sp", lng_t[:], lng_d, [], [("lng",)])
        dma("sp", lnb_t[:], lnb_d, [], [("lnb",)])
        dma("sp", es_t[:], esin_d, [], [("es",)])
        dma("sp", fsm_t[:, 0:4], fsm_d, [], [("fsm",)])
        dma("sp", tv_t[:], tv_d, [], [("tv",)])
        SKIP = XT.f32(0, 1024, 0, 1)
        ZROW = XT.bf(2048, 3072, 0, 1)
        dma("sp", SKIP.ap, skip_d, [], SKIP.keys)
        IDENT = Reg(ident_t[:], [("ident",)])
        aid_t = sb("aid_s", [128, 128], F32)
        dma("sp", aid_t[:], aid_d, [], [("aid",)])
        AID = Reg(aid_t[:], [("aid",)])
        jm_t = sb("jm_s", [128, 256], BF16)
        dma("sp", jm_t[:], jm_d, [], [("jm",)])
        JMAT = Reg(jm_t[:, 0:128], [("jm",)])
        NJMAT = Reg(jm_t[:, 128:256], [("jm",)])
        rot_t = sb("rot_s", [128, 64], F32)
        dma("sp", rot_t[:], rot_d, [], [("rot",)])
        memset("pool", Reg(ones_t[:, 0:64], [("ones",)]), 1.0)
        memset("pool", Reg(ones_t[:, 64:192], [("ones",)]), 0.0)
        memset("pool", Reg(ones_t[:, 192:256], [("ones",)]), 1.0)
        ONES = [Reg(ones_t[:, 0:128], [("ones",)]), Reg(ones_t[:, 128:256], [("ones",)])]
        memset("pool", ZROW, 0.0)
        act(Reg(es_t[:], [("es",)]), Reg(es_t[:], [("es",)]), AF.Exp)

        wstate = {"n": 0}

        def load_w_raw(idx):
            slot = wstate["n"] % NWB
            wstate["n"] += 1
            r = WB.bf(slot * 1024, slot * 1024 + 1024)
            dma("pool", r.ap.rearrange("p (a b) -> p a b", b=128), wst_d[idx], [], r.keys)
            return slot

        preloaded = {}

        def preload_w(idxs):
            for idx in idxs:
                if idx not in preloaded:
                    preloaded[idx] = load_w_raw(idx)

        def load_w(idx):
            if idx in preloaded:
                return preloaded.pop(idx)
            return load_w_raw(idx)

        def wchunk(slot, dc):
            return WB.bf(slot * 1024 + dc * 128, slot * 1024 + dc * 128 + 128)

        def wchunk_all(slot):
            return WB.bf(slot * 1024, slot * 1024 + 1024)

        def xTr(dc, t0, t1):
            return XT.bf(dc * 2048 + t0, dc * 2048 + t1)

        fstate = {"n": 0}


        def Fblk(b, s_c):
            return S3.bf(b * 1024 + s_c * 128, b * 1024 + s_c * 128 + 128)

        def load_F2(fc):
            b = fstate["n"] % 6
            fstate["n"] += 1
            r = S3.bf(b * 1024, b * 1024 + 1024)
            dma("sp", r.ap.rearrange("p (a b) -> p a b", b=128), F2_d[fc], [], r.keys)
            return b

        def rev(t, c0, n):
            return t[:, c0 + n - 1:(c0 - 1 if c0 > 0 else None):-1]

        def filter_phase():
            S2f = S2
            dma("sp", S2f.f32(0, 2240, 0, 64).ap, fwp_d, [], S2f.f32(0, 2240).keys)
            dma("sp", W32.f32(0, 4096, 0, 33).ap, fz_d, [], W32.f32(0, 4096).keys)
            dma("sp", S3.f32(0, 1024).ap, dl_d, [], S3.f32(0, 1024).keys)
            FSM = [("fsm",)]
            for l in range(3):
                ts("dve", Reg(fsm_t[:, 4 + l:5 + l], FSM), Reg(fsm_t[:, l:l + 1], FSM), fsm_t[:, 3:4], None, ALU.mult)
            W4B = XT.bf(4096, 6144, 0, 64)
            dma("pool", W4B.ap, fwp_d[:, 0:2048], [], W4B.keys)

            def H16(d):
                return XT.bf(6144 + d * 2048, 6144 + d * 2048 + 2048, 0, 64)
            W4 = lambda c0, c1: Reg(W4B.ap[:, c0:c1], W4B.keys)
            W1 = S2f.f32(2048, 2112, 0, 33)
            W2 = S2f.f32(2112, 2176, 0, 64)
            W3 = S2f.f32(2176, 2240, 0, 64)
            def Hb(d, c0, c1):
                return YH.f32(d * 2048 + c0, d * 2048 + c1, 0, 64)
            bank = {"n": 0}

            def nb():
                b = bank["n"] % 8
                bank["n"] += 1
                return b
            for layer in range(3):
                for d in range(2):
                    b0 = d * 4
                    for blk in range(4):
                        c0, c1 = blk * 512, blk * 512 + 512
                        if layer == 0:
                            mm(PS(b0 + blk, 0, 64), W1, W32.f32(d * 2048 + c0, d * 2048 + c1, 0, 33), True, True)
                        else:
                            mm(PS(b0 + blk, 0, 64), W2 if layer == 1 else W3, Hb(d, c0, c1), True, True)
                    TA = YH.f32(4096, 6144, 0, 64) if d == 0 else S1.f32(0, 2048, 0, 64)
                    TB = YH.f32(6144, 8192, 0, 64) if d == 0 else S1.f32(2048, 4096, 0, 64)
                    ts("dve", TA, PSW(b0, 4, 0, 64), fsm_t[:, 3:4], fsm_t[:, 4 + layer:5 + layer], ALU.mult, ALU.add,
                       extra_reads=FSM)
                    ts("dve", TB, TA, PI, -2.0 * PI, ALU.is_gt, ALU.mult)
                    tt("dve", TA, TA, TB, ALU.add)
                    ts("dve", TB, TA, -PI, 2.0 * PI, ALU.is_lt, ALU.mult)
                    tt("dve", TA, TA, TB, ALU.add)
                    if layer < 2:
                        act(Hb(d, 0, 2048), TA, AF.Sin)
                    else:
                        act(H16(d), TA, AF.Sin)
            bank["n"] = 0
            for tc in range(16):
                D1 = YH.f32(4096, 5120)
                D2 = YH.f32(5120, 6144)
                HF = YH.bf(12288, 13312)
                HB = YH.bf(13312, 14336)
                act(D1, S3.f32(0, 1024), AF.Exp, scale=tv_t[:, tc:tc + 1], extra_reads=[("tv",)])
                act(D2, S3.f32(0, 1024), AF.Exp, scale=tv_t[:, 16 + tc:17 + tc], extra_reads=[("tv",)])
                for d in range(2):
                    for half in range(2):
                        b = nb()
                        h16 = H16(d)
                        mm(PS(b), Reg(h16.ap[:, tc * 128:tc * 128 + 128], h16.keys),
                           W4(d * 1024 + half * 512, d * 1024 + half * 512 + 512), True, True)
                        dst = (HF if d == 0 else HB)
                        dstr = Reg(dst.ap[:, half * 512:half * 512 + 512], dst.keys)
                        dec = (D1 if d == 0 else D2)
                        decr = Reg(dec.ap[:, half * 512:half * 512 + 512], dec.keys)
                        tt("dve", dstr, PS(b), decr, ALU.mult)
                if tc == 0:
                    r0 = Reg(HF.ap[0:1, :], HF.keys)
                    tt("dve", r0, r0, SKIP, ALU.add)
                tt("dve", S1.bf(tc * 1024, tc * 1024 + 1024), HF, HB, ALU.add)
                tt("dve", YA.bf(tc * 1024, tc * 1024 + 1024), HF, HB, ALU.subtract)
            for X in (S1, YA):
                for dc in range(8):
                    for half in range(2):
                        hi = X.bf((8 + dc) * 1024 + half * 512, (8 + dc) * 1024 + half * 512 + 512)
                        lo = X.bf((7 - dc) * 1024 + half * 512, (7 - dc) * 1024 + half * 512 + 512)
                        be, bo = nb(), nb()
                        mm(PS(be), IDENT, hi, True, False)
                        mm(PS(be), JMAT, lo, False, True)
                        mm(PS(bo), IDENT, hi, True, False)
                        mm(PS(bo), NJMAT, lo, False, True)
                        act(hi, PS(be), AF.Identity)
                        cp("dve", lo, PS(bo))
            order = []
            for j in range(16):
                order += [j, 16 + j]
            stg = [S3.bf(6144, 7168), S3.bf(7168, 8192)]
            fbs = {}
            for n in range(6):
                fbs[n] = load_F2(order[n])
            ROT = [("rot",)]
            for j in range(16):
                src = S1 if (j < 8) else YA
                pb4 = (j % 2) * 4
                for part in range(2):
                    n = 2 * j + part
                    fb = fbs[n]
                    for half in range(2):
                        for dc in range(8):
                            slot = (8 + dc) if part == 0 else (7 - dc)
                            mm(PS(pb4 + part * 2 + half), Fblk(fb, dc),
                               src.bf(slot * 1024 + half * 512, slot * 1024 + half * 512 + 512), dc == 0, dc == 7)
                    if n + 6 < 32:
                        fbs[n + 6] = load_F2(order[n + 6])
                t0c = 4096 + (j % 2) * 2048
                VR, VI = PSW(pb4, 2), PSW(pb4 + 2, 2)
                TA_ = YH.f32(t0c, t0c + 1024)
                TB_ = YH.f32(t0c + 1024, t0c + 2048)
                U1 = W32.f32(0, 1024)
                U2 = W32.f32(1024, 2048)
                ts("dve", TA_, VR, rot_t[:, j * 4 + 0:j * 4 + 1], None, ALU.mult, extra_reads=ROT)
                ts("dve", TB_, VI, rot_t[:, j * 4 + 2:j * 4 + 3], None, ALU.mult, extra_reads=ROT)
                ts("dve", U1, VI, rot_t[:, j * 4 + 1:j * 4 + 2], None, ALU.mult, extra_reads=ROT)
                ts("dve", U2, VR, rot_t[:, j * 4 + 3:j * 4 + 4], None, ALU.mult, extra_reads=ROT)
                tt("dve", stg[0], TA_, U1, ALU.add)
                tt("dve", stg[1], TB_, U2, ALU.add)
                dma("sp", ksp_d[j, 0], stg[0].ap, stg[0].keys, [("ksp", j, 0)])
                if j == 0:
                    dma("sp", ksp_d[j, 2], stg[0].ap, stg[0].keys, [("ksp", j, 2)])
                dma("sp", ksp_d[j, 1], stg[1].ap, stg[1].keys, [("ksp", j, 1)])
                if j == 0:
                    dma("sp", ksp_d[0, 2, 0:1, :], stg[1].ap[0:1, :], stg[1].keys, [("ksp", 0, 2)])
                    dma("sp", ksp_d[0, 1, 0:1, :], ZROW.ap, ZROW.keys, [("ksp", 0, 1)])

        def hyena_pass(seq, hh, widx0, next_idxs):
            ACC = [YA.bf(0, 2048), YA.bf(2048, 4096), YA.bf(4096, 6144)]
            HG = YA.bf(6144, 8192)
            P0 = YA.bf(8192, 10242)
            P2 = YA.bf(10244, 12294)
            ZTS = [YA.bf(12296, 14344), W32.bf(0, 2048)]
            ZTT = [(YA.t, 12296), (W32.t, 0)]
            ZBS = [W32.bf(2048, 4096), W32.bf(4096, 6144)]
            preload_w([widx0 + k for k in range(4)])
            memset("pool", Reg(P0.ap[:, 0:1], P0.keys), 0.0)
            memset("pool", Reg(P2.ap[:, 2049:2050], P2.keys), 0.0)
            kinds = [1, 2, 0]
            pbank = {"n": 0}

            def nb():
                b = pbank["n"] % 8
                pbank["n"] += 1
                return b

            def proj_chunk(cc):
                ch = hh * 4 + cc
                ZT = ZTS[cc % 2]
                for ki in range(4):
                    slot = load_w(widx0 + cc * 4 + ki)
                    for tq in range(4):
                        b = nb()
                        for dc in range(8):
                            mm(PS(b), wchunk(slot, dc), xTr(dc, tq * 512, tq * 512 + 512), dc == 0, dc == 7)
                        if ki == 3:
                            act(Reg(HG.ap[:, tq * 512:tq * 512 + 512], HG.keys[tq:tq + 1]), PS(b), AF.Silu)
                        else:
                            k = kinds[ki]
                            base = (k * 8 + ch) * 3
                            A = ACC[ki]
                            act(Reg(A.ap[:, tq * 512:tq * 512 + 512], A.keys[tq:tq + 1]), PS(b), AF.Identity,
                                scale=cw_t[:, base + 1:base + 2], bias=cb_t[:, k * 8 + ch:k * 8 + ch + 1],
                                extra_reads=[("cw",), ("cb",)])
                            act(Reg(P0.ap[:, 1 + tq * 512:1 + tq * 512 + 512], P0.keys), PS(b), AF.Identity,
                                scale=cw_t[:, base:base + 1], extra_reads=[("cw",)])
                            act(Reg(P2.ap[:, 1 + tq * 512:1 + tq * 512 + 512], P2.keys), PS(b), AF.Identity,
                                scale=cw_t[:, base + 2:base + 3], extra_reads=[("cw",)])
                    if ki < 3:
                        A = ACC[ki]
                        tt("dve", A, A, Reg(P0.ap[:, 0:2048], P0.keys), ALU.add)
                        tt("dve", A, A, Reg(P2.ap[:, 2:2050], P2.keys), ALU.add)
                    if ki == 1:
                        tt("dve", ZT, ACC[0], ACC[1], ALU.mult)
                        ZB = ZBS[cc % 2]
                        zt_t, zt_c0 = ZTT[cc % 2]
                        hi = Reg(zt_t[:, zt_c0 + 1024:zt_c0 + 2048], ZT.keys)
                        lo = Reg(rev(zt_t, zt_c0, 1024), ZT.keys)
                        tt("dve", Reg(ZB.ap[:, 1024:2048], ZB.keys), hi, lo, ALU.add)
                        tt("dve", Reg(ZB.ap[:, 0:1024], ZB.keys), hi, lo, ALU.subtract)
                tt("dve", YH.bf(ch * 2048, ch * 2048 + 2048), ACC[2], HG, ALU.mult)

            def transposes(cc):
                ZT = ZBS[cc % 2]
                for g4 in range(4):
                    b = nb()
                    for j in range(4):
                        s_c = g4 * 4 + j
                        mm(PS(b, 0, 128, j * 128, j * 128 + 128),
                           Reg(ZT.ap[:, s_c * 128:s_c * 128 + 128], ZT.keys), IDENT, True, True)
                    o = S2.bfv(g4 * 2048, g4 * 2048 + 2048, 512, 0, 4, cc * 128, cc * 128 + 128)
                    i = Reg(ps[b][:, :].rearrange("p (a b) -> p a b", b=128), [("ps", b)])
                    cp("act", o, i)

            for cc in range(4):
                proj_chunk(cc)
                if cc >= 1:
                    transposes(cc - 1)
            transposes(3)

            if DEBUG and seq == 0 and hh == 0:
                dma("sp", dbg_d[3, :, 0:8192], S2.t[:, :], S2.keys(0, 8192), [("dbg", 3)])
            def load_KT(jj):
                k0 = 6144 + (jj % 2) * 1536
                kt = [S3.bf(k0 + t * 512, k0 + t * 512 + 512) for t in range(3)]
                for t in range(3 if jj == 0 else 2):
                    dma("sp", kt[t].ap, ksp_d[jj, t, :, hh * 512:hh * 512 + 512], [("ksp", jj, t)], kt[t].keys)
                if jj != 0:
                    kt[2] = kt[0]
                return kt
            forder = []
            for j in range(16):
                forder += [j, 16 + j]
            KTS = {0: load_KT(0), 1: load_KT(1)}
            pre_g = {}
            fbs = {n: load_F2(forder[n]) for n in range(6)}
            for j in range(16):
                KT = KTS[j]
                pb = (j % 4) * 2
                for part in range(2):
                    n = 2 * j + part
                    fb = fbs[n]
                    for dc in range(8):
                        slot = (8 + dc) if part == 0 else dc
                        mm(PS(pb + part), Fblk(fb, dc), S2.bf(slot * 512, slot * 512 + 512), dc == 0, dc == 7)
                    if n + 6 < 32:
                        fbs[n + 6] = load_F2(forder[n + 6])
                w0 = (j % 2) * 2048
                T = [W32.bf(w0 + t * 512, w0 + t * 512 + 512) for t in range(4)]
                tt("dve", T[0], PS(pb), KT[0], ALU.mult)
                tt("dve", T[1], PS(pb + 1), KT[1], ALU.mult)
                tt("dve", T[2], PS(pb), KT[1], ALU.mult)
                tt("dve", T[3], PS(pb + 1), KT[2], ALU.mult)
                tt("dve", S1.bf(j * 512, j * 512 + 512), T[0], T[1], ALU.subtract)
                tt("dve", S1.bf((16 + j) * 512, (16 + j) * 512 + 512), T[2], T[3], ALU.add)
                if j + 2 < 16:
                    KTS[j + 2] = load_KT(j + 2)
                if j == 12:
                    for m_ in range(2):
                        r_ = W32.bf(4096 + m_ * 2048, 4096 + m_ * 2048 + 2048)
                        dma("sp", r_.ap.rearrange("p (a b) -> p a b", b=512),
                            G2_d[0, 0, 0][:, m_ * 4:m_ * 4 + 4, :], [], r_.keys)
                        pre_g[m_] = r_

            preload_w(next_idxs)
            gstate = 0
            epi = 0
            for h2 in range(2):
                for kind in range(2):
                    for g4 in range(4):
                        grp, q4 = g4 // 2, g4 % 2
                        if gstate in pre_g:
                            gr = pre_g[gstate]
                        else:
                            gb = (gstate - 2) % 4
                            gr = S2.bf(gb * 2048, gb * 2048 + 2048)
                            dma("sp", gr.ap.rearrange("p (a b) -> p a b", b=512),
                                G2_d[h2, kind, grp][:, q4 * 4:q4 * 4 + 4, :], [], gr.keys)
                        gstate += 1
                        for f4 in range(4):
                            fcl = q4 * 4 + f4
                            fc = kind * 16 + grp * 8 + fcl
                            for cc in range(4):
                                mm(PS(kind * 4 + cc), S1.bf(fc * 512 + cc * 128, fc * 512 + cc * 128 + 128),
                                   Reg(gr.ap[:, f4 * 512:f4 * 512 + 512], gr.keys),
                                   g4 == 0 and f4 == 0, g4 == 3 and f4 == 3)
                    if kind == 0:
                        for cc in range(4):
                            act(W32.f32(cc * 512, cc * 512 + 512), PS(cc), AF.Identity)
                for cc in range(4):
                    ch = hh * 4 + cc
                    YE = W32.f32(cc * 512, cc * 512 + 512)
                    w0 = 2048 + (epi % 2) * 1024
                    epi += 1
                    TP = W32.f32(w0, w0 + 512)
                    TM = W32.f32(w0 + 512, w0 + 1024)
                    tt("dve", TP, PS(4 + cc), YE, ALU.add)
                    tt("dve", TM, YE, PS(4 + cc), ALU.subtract)
                    rp = YH.bf(ch * 2048 + 1024 + h2 * 512, ch * 2048 + 1024 + h2 * 512 + 512)
                    tt("dve", rp, TP, rp, ALU.mult)
                    lo = ch * 2048 + 512 - h2 * 512
                    rm = YH.bf(lo, lo + 512)
                    tmr = Reg(rev(W32.t32, w0 + 512, 512), TM.keys)
                    tt("dve", rm, tmr, rm, ALU.mult)

        def attention_half(seq, gp, widx0, next_idxs):
            KZ = [S2.bf(0, 2048), S2.bf(2048, 4096)]
            VZ0 = 4096
            QT0 = 0
            AG0 = 8192
            BT0 = 0
            PT0 = 3072
            preload_w([widx0 + k for k in range(4)])
            memset("dve", S2.bf(0, 2048, 64, 128), 0.0)
            memset("dve", S2.bf(2048, 4096, 0, 64), 0.0)
            memset("dve", S2.bf(4096, 8192), 1.0)
            for e in range(2):
                for rel in range(3):
                    r = S3.bf(BT0 + (e * 3 + rel) * 512, BT0 + (e * 3 + rel) * 512 + 512)
                    dma("pool", r.ap, bt_d[gp, e, rel], [], r.keys)
            pbank = {"n": 0}

            def nb():
                b = pbank["n"] % 8
                pbank["n"] += 1
                return b
            slot = load_w(widx0 + 0)
            for tq in range(4):
                b = nb()
                for dc in range(8):
                    mm(PS(b), wchunk(slot, dc), xTr(dc, tq * 512, tq * 512 + 512), dc == 0, dc == 7)
                act(S2.bf(tq * 512, tq * 512 + 512, 0, 64), PS(b, 0, 64), AF.Identity)
                act(S2.bf(2048 + tq * 512, 2048 + tq * 512 + 512, 64, 128), PS(b, 64, 128), AF.Identity)
            slot = load_w(widx0 + 1)
            for g4 in range(4):
                b = nb()
                for j in range(4):
                    tk = g4 * 4 + j
                    for dc in range(8):
                        mm(PS(b, 0, 128, j * 128, j * 128 + 128), xTr(dc, tk * 128, tk * 128 + 128),
                           wchunk(slot, dc), dc == 0, dc == 7)
                c0 = VZ0 + g4 * 1024
                pv = ps[b][:, :].rearrange("p (a b) -> p a b", b=128)
                o0 = S2.bfv(c0, c0 + 1024, 256, 0, 4, 0, 64)
                o1 = S2.bfv(c0, c0 + 1024, 256, 0, 4, 192, 256)
                cp("act", o0, Reg(pv[:, :, 0:64], [("ps", b)]))
                cp("act", o1, Reg(pv[:, :, 64:128], [("ps", b)]))
            for i in range(4):
                slot = load_w(widx0 + 2 + i)
                for tq in range(4):
                    b = nb()
                    for dc in range(8):
                        mm(PS(b), wchunk(slot, dc), xTr(dc, tq * 512, tq * 512 + 512), dc == 0, dc == 7)
                    c = QT0 + i * 2048 + tq * 512
                    act(S1.bf(c, c + 512), PS(b), AF.Identity, scale=0.125)
            for i in range(4):
                slot = load_w(widx0 + 6 + i)
                for tq in range(4):
                    b = nb()
                    for dc in range(8):
                        mm(PS(b), wchunk(slot, dc), xTr(dc, tq * 512, tq * 512 + 512), dc == 0, dc == 7)
                    c = AG0 + i * 2048 + tq * 512
                    act(S1.bf(c, c + 512), PS(b), AF.Silu)

            ptn = {"n": 0}
            sbn = {"n": 0}

            def qk_exp(qb):
                tiles = []
                for e in range(2):
                    for rel in (-1, 0, 1):
                        kb = qb + rel
                        if kb < 0 or kb > 15:
                            continue
                        b = sbn["n"] % 4
                        sbn["n"] += 1
                        qv = S1.bfv(QT0, QT0 + 8192, 2048, 0, 4, qb * 128, qb * 128 + 128)
                        mm(PS(b), Reg(KZ[e].ap[:, kb * 128:kb * 128 + 128], KZ[e].keys), qv, True, False)
                        br = S3.bf(BT0 + (e * 3 + rel + 1) * 512, BT0 + (e * 3 + rel + 1) * 512 + 512)
                        mm(PS(b), IDENT, br, False, True)
                        pslot = ptn["n"] % 12
                        ptn["n"] += 1
                        pt = S3.bf(PT0 + pslot * 512, PT0 + pslot * 512 + 512)
                        act(pt, PS(b), AF.Exp)
                        tiles.append((e, kb, pt))
                return tiles

            def pv_norm(qb, tiles):
                tb = 4 + (qb % 2) * 2
                for e in range(2):
                    te = [t for t in tiles if t[0] == e]
                    for idx, (_, kb, pt) in enumerate(te):
                        vz = S2.bf(VZ0 + kb * 256 + e * 128, VZ0 + kb * 256 + e * 128 + 128)
                        mm(PS(tb + e), vz, pt, idx == 0, idx == len(te) - 1)
                w0 = (qb % 2) * 1024
                T0 = W32.f32(w0, w0 + 512)
                T1 = W32.f32(w0 + 512, w0 + 1024)
                ESK = [("es",)]
                c0 = gp * 512
                tt("dve", T0.rows(0, 64), PS(tb, 64, 128), Reg(es_t[64:128, c0:c0 + 512], ESK), ALU.add)
                tt("dve", T0.rows(64, 128), PS(tb + 1, 0, 64), Reg(es_t[0:64, c0:c0 + 512], ESK), ALU.add)
                act(T0, T0, AF.Ln)
                act(T0, T0, AF.Exp, scale=-1.0)
                tt("dve", T1.rows(0, 64), PS(tb, 0, 64), T0.rows(0, 64), ALU.mult)
                tt("dve", T1.rows(64, 128), PS(tb + 1, 64, 128), T0.rows(64, 128), ALU.mult)
                o = YA.bfv(gp * 8192, gp * 8192 + 8192, 2048, 0, 4, qb * 128, qb * 128 + 128)
                ag = S1.bfv(AG0, AG0 + 8192, 2048, 0, 4, qb * 128, qb * 128 + 128)
                t1v = Reg(W32.t32[:, w0 + 512:w0 + 1024].rearrange("p (a b) -> p a b", b=128), T1.keys)
                tt("dve", o, t1v, ag, ALU.mult)

            prev = None
            for qb in range(16):
                if qb == 11:
                    preload_w(next_idxs)
                tiles = qk_exp(qb)
                if prev is not None:
                    pv_norm(qb - 1, prev)
                prev = tiles
            pv_norm(15, prev)

        def tail(seq, widx0, after_part1, mid_part2):
            def load_tw(idx, slot):
                r = S3.bf(slot * 1024, slot * 1024 + 1024)
                dma("pool", r.ap.rearrange("p (a b) -> p a b", b=128), wst_d[idx], [], r.keys)

            def tw(slot, dc):
                if isinstance(slot, tuple):
                    return wchunk(slot[1], dc)
                return S3.bf(slot * 1024 + dc * 128, slot * 1024 + dc * 128 + 128)

            first = [("wb", load_w(widx0 + k)) for k in range(4)]
            wr = S2.bf(0, 8192)
            dma("pool", wr.ap.rearrange("p (a b) -> p a b", b=1024), wout_d, [], wr.keys)
            pb = 0
            for oc in range(8):
                if oc + 1 < 8:
                    for k in range(4):
                        load_tw(widx0 + (oc + 1) * 4 + k, ((oc + 1) % 2) * 4 + k)
                slots = first if oc == 0 else [(oc % 2) * 4 + k for k in range(4)]
                for tq in range(4):
                    base = (pb % 2) * 4
                    w0 = (pb % 2) * 2048
                    pb += 1
                    for k in range(4):
                        for dc in range(8):
                            if k < 2:
                                rhs = xTr(dc, tq * 512, tq * 512 + 512)
                            elif k == 2:
                                rhs = YA.bf(dc * 2048 + tq * 512, dc * 2048 + tq * 512 + 512)
                            else:
                                rhs = YH.bf(dc * 2048 + tq * 512, dc * 2048 + tq * 512 + 512)
                            mm(PS(base + k), tw(slots[k], dc), rhs, dc == 0, dc == 7)
                    GA = W32.f32(w0, w0 + 512)
                    GH = W32.f32(w0 + 512, w0 + 1024)
                    M1 = W32.f32(w0 + 1024, w0 + 1536)
                    M2 = W32.f32(w0 + 1536, w0 + 2048)
                    act(GA, PS(base + 0), AF.Sigmoid)
                    act(GH, PS(base + 1), AF.Sigmoid)
                    tt("dve", M1, PS(base + 2), GA, ALU.mult)
                    tt("dve", M2, PS(base + 3), GH, ALU.mult)
                    tt("pool", S1.bf(oc * 2048 + tq * 512, oc * 2048 + tq * 512 + 512), M1, M2, ALU.add)
            if DEBUG and seq == 0:
                dma("sp", dbg_d[2], S1.t[:, :], S1.keys(0, 16384), [("dbg", 2)])
            def XRb(tk):
                k = tk % 4
                return YH.f32(k * 1024, k * 1024 + 1024)

            def Rb(tk):
                k = tk % 4
                return YH.f32(4096 + k * 1024, 4096 + k * 1024 + 1024)

            def load_xr(tk):
                row0 = seq * S + tk * 128
                dma("sp", XRb(tk).ap, xr_d[row0:row0 + 128, :], [], XRb(tk).keys)

            for tk in range(3):
                load_xr(tk)
            after_part1()

            def stage1a(tk):
                base = (tk % 4) * 2
                XR = XRb(tk)
                for half in range(2):
                    for dc in range(8):
                        mm(PS(base + half), S1.bf(dc * 2048 + tk * 128, dc * 2048 + tk * 128 + 128),
                           S2.bf(dc * 1024 + half * 512, dc * 1024 + half * 512 + 512), dc == 0, False)
                    xs = Reg(XR.ap[:, half * 512:half * 512 + 512], XR.keys)
                    mm(PS(base + half), AID, xs, False, True)
                so = (tk % 4) * 16
                ST = [("st", tk % 4)]
                PR = PSW(base, 2)
                JK = Rb(tk)
                sc.add("act", lambda e_, o=JK.ap, i=PR.ap, a=st_t[:, so + 0:so + 1]:
                       e_.activation(out=o, in_=i, func=AF.Identity, accum_out=a),
                       reads=PR.keys, writes=JK.keys + ST)
                sc.add("act", lambda e_, o=JK.ap, i=PR.ap, a=st_t[:, so + 1:so + 2]:
                       e_.activation(out=o, in_=i, func=AF.Square, accum_out=a),
                       reads=PR.keys, writes=JK.keys + ST)

            def stage1b(tk):
                so = (tk % 4) * 16
                ST = [("st", tk % 4)]
                STR = lambda c: Reg(st_t[:, so + c:so + c + 1], ST)
                ts("dve", STR(12), STR(0), 1.0 / D, None, ALU.mult)
                tt("dve", STR(3), STR(12), STR(12), ALU.mult)
                stt(STR(13), STR(1), 1.0 / D, STR(3), ALU.mult, ALU.subtract)
                ts("dve", STR(13), STR(13), LN_EPS, None, ALU.add)
                act(STR(14), STR(13), AF.Sqrt)
                sc.add("dve", lambda e_, o=st_t[:, so + 14:so + 15]: e_.reciprocal(out=o, in_=o), reads=ST, writes=ST)
                ts("dve", STR(15), STR(12), st_t[:, so + 14:so + 15], -1.0, ALU.mult, ALU.mult)

            def stage2(tk):
                base = (tk % 4) * 2
                R = Rb(tk)
                so = (tk % 4) * 16
                ST = [("st", tk % 4)]
                row0 = seq * S + tk * 128
                act(R, PSW(base, 2), AF.Identity, scale=st_t[:, so + 14:so + 15], bias=st_t[:, so + 15:so + 16],
                    extra_reads=ST)
                tt("dve", R, R, Reg(lng_t[:], [("lng",)]), ALU.mult)
                tt("dve", R, R, Reg(lnb_t[:], [("lnb",)]), ALU.add)
                dma("pool", out_d[row0:row0 + 128, :], R.ap, R.keys, [("out", row0)])

            for tk in range(16):
                if tk + 3 < 16:
                    load_xr(tk + 3)
                stage1a(tk)
                if tk >= 1:
                    stage1b(tk - 1)
                if tk >= 2:
                    stage2(tk - 2)
                if tk == 8:
                    mid_part2()
            stage1b(15)
            stage2(14)
            stage2(15)

        def load_xT(seq, dcs=range(8)):
            for dc in dcs:
                r = XT.bf(dc * 2048, dc * 2048 + 2048)
                dma("pool", r.ap, xT_d[seq, :, dc, :], [], r.keys)

        def load_xT_split_a(seq):
            for dc in range(4):
                r = YA.f32(dc * 2048, dc * 2048 + 2048)
                dma("sp", r.ap, xT_d[seq, :, dc, :], [], r.keys)
            load_xT(seq, range(4, 8))

        def load_xT_split_b(seq):
            for dc in range(4):
                cp("dve", XT.bf(dc * 2048, dc * 2048 + 2048), YA.f32(dc * 2048, dc * 2048 + 2048))

        filter_phase()
        load_xT(0)
        for seq in range(NSEQ):
            hyena_pass(seq, 0, 0, [16, 17, 18, 19])
            hyena_pass(seq, 1, 16, [32, 33, 34, 35])
            if DEBUG and seq == 0:
                dma("sp", dbg_d[0], YH.t[:, :], YH.keys(0, 16384), [("dbg", 0)])
            attention_half(seq, 0, 32, [42, 43, 44, 45])
            attention_half(seq, 1, 42, [52, 53, 54, 55])
            if DEBUG and seq == 0:
                dma("sp", dbg_d[1], YA.t[:, :], YA.keys(0, 16384), [("dbg", 1)])
            tail(seq, 52, (lambda s_=seq: (load_xT_split_a(s_ + 1), preload_w([0, 1, 2, 3]))) if seq + 1 < NSEQ else (lambda: None),
                 (lambda s_=seq: load_xT_split_b(s_ + 1)) if seq + 1 < NSEQ else (lambda: None))

        sems = {e: es.enter_context(nc.semaphore(f"sem_{e}")) for e in ("pe", "act", "dve", "pool")}
        dma_sems = {
            "sp": [es.enter_context(nc.semaphore(f"dsp{i}")) for i in range(24)],
            "pool": [es.enter_context(nc.semaphore(f"dpl{i}")) for i in range(12)],
            "act": [], "pe": [], "dve": [],
        }
        block = es.enter_context(nc.Block())
        sc.emit(nc, block, sems, dma_sems)
    return nc


def _t5_bucket(rel):
    half = 16
    max_exact = 8
    ret = (rel > 0).astype(np.int32) * half
    n = np.abs(rel)
    n_safe = np.maximum(n, 1).astype(np.float32)
    large = max_exact + (np.log(n_safe / max_exact) / math.log(128 / max_exact) * (half - max_exact)).astype(np.int32)
    large = np.minimum(large, half - 1)
    return (ret + np.where(n < max_exact, n, large)).astype(np.int32)


def _wchunk(W, cols):
    return np.ascontiguousarray(W[:, cols].reshape(8, 128, 128).transpose(1, 0, 2))


_CONST = {}


def _constants():
    if _CONST:
        return _CONST
    N = 2 * S
    perm = np.concatenate([np.arange(0, S, 2), np.arange(1, S, 2)]).astype(np.float64)
    s = np.arange(S, dtype=np.float64)[:, None]
    ang = 2 * np.pi * perm[None, :] * s / N
    F = np.concatenate([np.cos(ang), -np.sin(ang)], axis=1)
    F[:, S] = (-1.0) ** np.arange(S)
    G = np.concatenate([2.0 / N * np.cos(ang.T), -2.0 / N * np.sin(ang.T)], axis=0)
    G[0, :] = 1.0 / N
    G[S, :] = (1.0 / N) * (-1.0) ** np.arange(S)
    H = S // 2
    theta = 2 * np.pi * perm / N
    jj = np.arange(H, dtype=np.float64) + 0.5
    Fc = np.cos(jj[:, None] * theta[None, :])
    Fs = -np.sin(jj[:, None] * theta[None, :])
    Fs[:, 0] = -np.sin(np.pi * jj)
    F2 = np.concatenate([Fc, Fs], axis=1)
    F2h = F2.reshape(8, 128, 32, 128).transpose(2, 1, 0, 3)
    _CONST["F2"] = np.ascontiguousarray(F2h).astype(ml_dtypes.bfloat16)
    Gc = (2.0 / N) * np.cos(theta[:, None] * jj[None, :])
    Gc[0, :] = 1.0 / N
    Gs = -(2.0 / N) * np.sin(theta[:, None] * jj[None, :])
    Gs[0, :] = -(1.0 / N) * np.sin(np.pi * jj)
    G2 = np.stack([Gc, Gs], axis=0)
    G2h = G2.reshape(2, 2, 8, 128, 2, 512).transpose(4, 0, 1, 3, 2, 5)
    _CONST["G2"] = np.ascontiguousarray(G2h).astype(ml_dtypes.bfloat16)
    Jm = np.eye(128, dtype=np.float32)[::-1]
    _CONST["jm"] = np.ascontiguousarray(np.concatenate([Jm, -Jm], axis=1)).astype(ml_dtypes.bfloat16)
    phi = np.mod(theta * 1023.5, 2 * np.pi)
    cr, sr, ci, nsi = np.cos(phi), np.sin(phi), np.cos(phi).copy(), -np.sin(phi)
    cr[0], sr[0], ci[0], nsi[0] = 1.0, 0.0, -1.0, 0.0
    rot = np.stack([cr, sr, ci, nsi], axis=1).reshape(16, 128, 4).transpose(1, 0, 2).reshape(128, 64)
    _CONST["rot"] = np.ascontiguousarray(rot).astype(np.float32)
    _CONST["ident"] = np.eye(128, dtype=np.float32).astype(ml_dtypes.bfloat16)
    _CONST["aid"] = (np.eye(128, dtype=np.float32) * np.float32(ALPHA)).astype(np.float32)
    f32 = np.float32
    bands = 16
    t = np.linspace(0.0, 1.0, S, dtype=f32)[:, None]
    w = (2.0 * math.pi * np.arange(S, dtype=f32)[:, None] / S).astype(f32)
    fb = np.linspace(1e-4, bands - 1, bands, dtype=f32)[None, :]
    z = np.concatenate([t, np.cos(fb * w), -np.sin(fb * w)], axis=-1).astype(f32)
    zr = np.empty_like(z)
    zr[1:] = z[:0:-1]
    zr[0] = z[0]
    _CONST["fz"] = np.ascontiguousarray(np.concatenate([z.T, zr.T], axis=1)).astype(f32)
    max_decay = math.log(1e-2) / 0.3
    min_decay = math.log(1e-2) / 1.5
    deltas = np.abs(np.linspace(min_decay, max_decay, 1024, dtype=f32))
    _CONST["dl"] = np.ascontiguousarray(np.broadcast_to(deltas[None, :], (128, 1024))).astype(f32)
    tvals = t[:, 0]
    tv = np.zeros((128, 32), f32)
    tv[:, 0:16] = -tvals.reshape(16, 128).T
    trev = np.empty(S, f32)
    trev[1:] = tvals[:0:-1]
    trev[0] = 1e4
    tv[:, 16:32] = -trev.reshape(16, 128).T
    _CONST["tv"] = tv
    key = np.arange(128)[:, None]
    q = np.arange(128)[None, :]
    geo = []
    for rel in (-1, 0, 1):
        rp = rel * 128 + key - q
        geo.append((_t5_bucket(rp), np.abs(rp) <= 128))
    _CONST["geo"] = geo
    return _CONST


def _feat_order():
    order = []
    for gp in range(2):
        for i in range(4):
            h0 = 4 * (2 * gp) + i
            h1 = 4 * (2 * gp + 1) + i
            order.append(np.concatenate([h0 * 64 + np.arange(64), h1 * 64 + np.arange(64)]))
    return order


def _prep_shared(inp):
    C = _constants()
    f32 = np.float32
    w_in = np.asarray(inp["w_in"], f32)[0]
    Wa = np.asarray(inp["w_branch_attn"], f32)[0]
    Wh = np.asarray(inp["w_branch_hyena"], f32)[0]
    Wo = np.asarray(inp["w_out"], f32)[0]
    forder = _feat_order()
    chunks = []
    ar = np.arange(128)
    for hh in range(2):
        for cc in range(4):
            ch = hh * 4 + cc
            chunks.append(_wchunk(w_in, 2560 + 1024 + ch * 128 + ar))
            chunks.append(_wchunk(w_in, 2560 + 2048 + ch * 128 + ar))
            chunks.append(_wchunk(w_in, 2560 + ch * 128 + ar))
            chunks.append(_wchunk(w_in, 5632 + ch * 128 + ar))
    for gp in range(2):
        chunks.append(_wchunk(w_in, 1024 + gp * 128 + ar))
        chunks.append(_wchunk(w_in, 1280 + gp * 128 + ar))
        for i in range(4):
            chunks.append(_wchunk(w_in, forder[gp * 4 + i]))
        for i in range(4):
            chunks.append(_wchunk(w_in, 1536 + forder[gp * 4 + i]))
    Wa_p = Wa[np.concatenate(forder), :]
    for oc in range(8):
        chunks.append(_wchunk(w_in, 6656 + oc * 128 + ar))
        chunks.append(_wchunk(w_in, 7680 + oc * 128 + ar))
        chunks.append(_wchunk(Wa_p, oc * 128 + ar))
        chunks.append(_wchunk(Wh, oc * 128 + ar))
    wst = np.ascontiguousarray(np.stack(chunks, axis=0))
    assert wst.shape == (NW, 128, 8, 128)
    wout = np.ascontiguousarray(Wo.reshape(8, 128, 1024).transpose(1, 0, 2))
    conv_w = np.asarray(inp["conv_w"], f32)[0]
    conv_b = np.asarray(inp["conv_b"], f32)[0]
    cw = np.ascontiguousarray(conv_w.reshape(3, 3, 8, 128).transpose(3, 1, 2, 0).reshape(128, 72))
    cb = np.ascontiguousarray(conv_b.reshape(3, 8, 128).transpose(2, 0, 1).reshape(128, 24))
    sink = np.asarray(inp["attn_sink"], f32)[0]
    rel_bias = np.asarray(inp["rel_bias"], f32)
    esin = np.zeros((128, 2, 4, 128), f32)
    bt = np.full((2, 2, 3, 128, 4, 128), NEG, f32)
    for gp in range(2):
        for e in range(2):
            for i in range(4):
                h = 4 * (2 * gp + e) + i
                esin[(1 - e) * 64:(2 - e) * 64, gp, i, :] = sink[h]
                for r in range(3):
                    bucket, valid = C["geo"][r]
                    bt[gp, e, r, :, i, :] = np.where(valid, rel_bias[bucket, h], f32(NEG))
    shared = {
        "wst": wst, "wout": wout, "cw": cw, "cb": cb,
        "esin": np.ascontiguousarray(esin.reshape(128, 1024)),
        "bt": np.ascontiguousarray(bt.reshape(2, 2, 3, 128, 512)),
        "lng": np.ascontiguousarray(np.broadcast_to(np.asarray(inp["ln_g"], f32)[0][None, :], (128, D))),
        "lnb": np.ascontiguousarray(np.broadcast_to(np.asarray(inp["ln_b"], f32)[0][None, :], (128, D))),
        "fz": C["fz"], "dl": C["dl"], "tv": C["tv"],
        "F2": C["F2"], "G2": C["G2"], "ident": C["ident"], "aid": C["aid"], "jm": C["jm"], "rot": C["rot"],
        "skip": np.ascontiguousarray(np.asarray(inp["hyena_skip"], f32)[0][None, :]),
    }
    fwp = np.zeros((64, 2240), f32)
    fwp[:, 0:2048] = np.asarray(inp["filt_w4"], f32)[0]
    fwp[0:33, 2048:2112] = np.asarray(inp["filt_w1"], f32)[0]
    fwp[:, 2112:2176] = np.asarray(inp["filt_w2"], f32)[0]
    fwp[:, 2176:2240] = np.asarray(inp["filt_w3"], f32)[0]
    shared["fwp"] = fwp
    shared["fsm"] = np.ascontiguousarray(np.stack([
        np.asarray(inp["filt_b1"], f32)[0], np.asarray(inp["filt_b2"], f32)[0],
        np.asarray(inp["filt_b3"], f32)[0], np.asarray(inp["filt_freq"], f32)[0]], axis=1))
    return shared


_PROG = {}


def kernel(**inputs):
    x = np.asarray(inputs["x"], np.float32)
    shared = _prep_shared(inputs)
    if "nc" not in _PROG:
        _PROG["nc"] = build_program()
    nc = _PROG["nc"]
    in_maps = []
    for c in range(NCORE):
        xc = x[2 * c:2 * c + 2]
        xT = np.ascontiguousarray(xc.transpose(0, 2, 1).reshape(2, 8, 128, S).transpose(0, 2, 1, 3))
        m = dict(shared)
        m["xT"] = xT
        m["xr"] = np.ascontiguousarray(xc.reshape(2 * S, D))
        in_maps.append(m)
    res = run_bass_kernel_spmd(nc, in_maps, core_ids=list(range(NCORE)))
    out = np.stack([np.asarray(r["out"], np.float32).reshape(2, S, D) for r in res.results], axis=0)
    return out.reshape(16, S, D)
```
